# Optimizing a Trainium2 kernel written in Bass

```python
import jax, jax.numpy as jnp
from jax import lax
import numpy as np

D_MODEL = 1024
BATCH = 8
SEQ = 2048
DEPTH = 2

CHUNK = 64
Q_BLOCK = 128
BRANCH_WIDTH = D_MODEL // 2
HGRN_HEADS = 4
HGRN_HEAD_DIM = BRANCH_WIDTH // HGRN_HEADS
CONV_CH = BRANCH_WIDTH
CONV_WIDTH = 31
SB_HEADS = 8
SB_HEAD_DIM = BRANCH_WIDTH // SB_HEADS
N_BRANCH = 3
EPS = 1e-6
TINY = 1e-30
SPLIT_SIZES = (BRANCH_WIDTH, BRANCH_WIDTH, BRANCH_WIDTH, BRANCH_WIDTH,
               2 * CONV_CH, CONV_CH,
               BRANCH_WIDTH, BRANCH_WIDTH, BRANCH_WIDTH, BRANCH_WIDTH,
               N_BRANCH * D_MODEL)
IN_COLS = 11 * BRANCH_WIDTH + N_BRANCH * D_MODEL

kernel_name = "hybrid_gated_hgrn2_conformer_stickbreaking"


def rms_norm(x, w):
    xf = x.astype(jnp.float32)
    y = xf * lax.rsqrt(jnp.mean(xf * xf, axis=-1, keepdims=True) + EPS)
    return (y * w.astype(jnp.float32)).astype(x.dtype)


def layer_norm(x, w, b):
    xf = x.astype(jnp.float32)
    mu = jnp.mean(xf, axis=-1, keepdims=True)
    var = jnp.mean(jnp.square(xf - mu), axis=-1, keepdims=True)
    y = (xf - mu) * lax.rsqrt(var + EPS)
    return (y * w.astype(jnp.float32) + b.astype(jnp.float32)).astype(x.dtype)


def hgrn2_chunkwise(q, k, v, log_f):
    B, S, H, K = q.shape
    V = v.shape[-1]
    N = S // CHUNK

    def to_chunks(t):
        return t.reshape(B, N, CHUNK, H, t.shape[-1]).transpose(1, 0, 3, 2, 4)

    causal = jnp.tril(jnp.ones((CHUNK, CHUNK), dtype=bool))[None, None, :, :, None]

    def step(state, inp):
        q_c, k_c, v_c, g_c = inp
        b = jnp.cumsum(g_c, axis=2)
        b_last = b[:, :, -1:, :]
        o_inter = jnp.einsum('bhtk,bhkv->bhtv', q_c * jnp.exp(b), state)
        diff = b[:, :, :, None, :] - b[:, :, None, :, :]
        decay = jnp.where(causal, jnp.exp(jnp.where(causal, diff, 0.0)), 0.0)
        attn = jnp.einsum('bhtk,bhsk,bhtsk->bhts', q_c, k_c, decay)
        o = o_inter + jnp.einsum('bhts,bhsv->bhtv', attn, v_c)
        new_state = (jnp.exp(b_last[:, :, 0, :])[..., None] * state
                     + jnp.einsum('bhsk,bhsv->bhkv', k_c * jnp.exp(b_last - b), v_c))
        return new_state, o

    state0 = jnp.zeros((B, H, K, V), jnp.float32)
    _, o = lax.scan(step, state0, (to_chunks(q), to_chunks(k), to_chunks(v), to_chunks(log_f)))
    return o.transpose(1, 0, 3, 2, 4).reshape(B, S, H, V)


def hgrn2_branch(q_raw, f_raw, i_raw, z, lb, norm_w):
    B, S, _ = q_raw.shape
    shp = (B, S, HGRN_HEADS, HGRN_HEAD_DIM)
    q = jax.nn.silu(q_raw.astype(jnp.float32)) * (HGRN_HEAD_DIM ** -0.5)
    fr = f_raw.astype(jnp.float32)
    lbf = lb.astype(jnp.float32)
    f = lbf + (1.0 - lbf) * jax.nn.sigmoid(fr)
    log_f = jnp.log(jnp.maximum(f, TINY))
    k = (1.0 - lbf) * jax.nn.sigmoid(-fr)
    v = i_raw.astype(jnp.float32)
    o = hgrn2_chunkwise(q.reshape(shp), k.reshape(shp), v.reshape(shp), log_f.reshape(shp))
    o = rms_norm(o, norm_w) * jax.nn.silu(z.astype(jnp.float32).reshape(shp))
    return o.reshape(B, S, BRANCH_WIDTH).astype(q_raw.dtype)


def conformer_conv_branch(glu_in, z, conv_w, conv_b, ln_w, ln_b):
    a, g = jnp.split(glu_in, 2, axis=-1)
    u = a * jax.nn.sigmoid(g)
    y = lax.conv_general_dilated(u, conv_w[:, None, :].astype(u.dtype), window_strides=(1,),
                                 padding=((CONV_WIDTH - 1, 0),),
                                 dimension_numbers=('NWC', 'WIO', 'NWC'),
                                 feature_group_count=CONV_CH) + conv_b
    y = jax.nn.silu(layer_norm(y, ln_w, ln_b))
    return y * jax.nn.silu(z)


def stick_breaking_branch(q_raw, k_raw, v_raw, z):
    B, S, _ = q_raw.shape
    shp = (B, S, SB_HEADS, SB_HEAD_DIM)
    q = q_raw.astype(jnp.float32).reshape(shp)
    k = k_raw.astype(jnp.float32).reshape(shp)
    v = v_raw.astype(jnp.float32).reshape(shp)
    scale = SB_HEAD_DIM ** -0.5
    outs = []
    for blk in range(S // Q_BLOCK):
        t0, t1 = blk * Q_BLOCK, (blk + 1) * Q_BLOCK
        logits = jnp.einsum('bthd,bshd->bhts', q[:, t0:t1], k[:, :t1]) * scale
        mask = jnp.arange(t1)[None, :] < jnp.arange(t0, t1)[:, None]
        log_beta = jax.nn.log_sigmoid(logits)
        log_1m = jnp.where(mask, jax.nn.log_sigmoid(-logits), 0.0)
        after = lax.cumsum(log_1m, axis=3, reverse=True) - log_1m
        weights = jnp.where(mask, jnp.exp(jnp.where(mask, log_beta + after, 0.0)), 0.0)
        outs.append(jnp.einsum('bhts,bshd->bthd', weights, v[:, :t1]))
    o = jnp.concatenate(outs, axis=1).reshape(B, S, BRANCH_WIDTH)
    return (o * jax.nn.silu(z.astype(jnp.float32))).astype(q_raw.dtype)


def setup_inputs(seed: int = 0) -> dict:
    key = jax.random.key(seed)
    ks = jax.random.split(key, 16)
    D, W = D_MODEL, BRANCH_WIDTH
    f32 = jnp.float32
    nrm = lambda k, shp, s: jax.random.normal(k, shp, f32) * s
    return {
        "x": nrm(ks[0], (BATCH, SEQ, D), 1.0),
        "c": nrm(ks[1], (BATCH, D), 1.0),
        "ada_w": nrm(ks[2], (DEPTH, D, 3 * D), 0.5 * D ** -0.5),
        "ada_b": nrm(ks[3], (DEPTH, 3 * D), 0.02),
        "norm_w": 1.0 + nrm(ks[4], (DEPTH, D), 0.02),
        "w_in": nrm(ks[5], (DEPTH, D, IN_COLS), D ** -0.5),
        "hgrn_lb": nrm(ks[6], (DEPTH, W), 0.1),
        "hgrn_norm_w": 1.0 + nrm(ks[7], (DEPTH, HGRN_HEAD_DIM), 0.02),
        "conv_w": nrm(ks[8], (DEPTH, CONV_WIDTH, CONV_CH), CONV_WIDTH ** -0.5),
        "conv_b": nrm(ks[9], (DEPTH, CONV_CH), 0.02),
        "conv_ln_w": 1.0 + nrm(ks[10], (DEPTH, CONV_CH), 0.02),
        "conv_ln_b": nrm(ks[11], (DEPTH, CONV_CH), 0.02),
        "w_branch": nrm(ks[12], (DEPTH, N_BRANCH, W, D), W ** -0.5),
        "w_out": nrm(ks[13], (DEPTH, D, D), D ** -0.5),
        "final_norm_w": 1.0 + nrm(ks[14], (D,), 0.02),
    }


def reference(x, c, ada_w, ada_b, norm_w, w_in, hgrn_lb, hgrn_norm_w, conv_w, conv_b,
              conv_ln_w, conv_ln_b, w_branch, w_out, final_norm_w):
    B, S, D = x.shape
    lb_soft = jax.nn.softmax(hgrn_lb.astype(jnp.float32), axis=0)
    lower_bounds = jnp.cumsum(lb_soft, axis=0) - lb_soft[0:1]
    c_act = jax.nn.silu(c)
    split_at = [int(v) for v in np.cumsum(SPLIT_SIZES)[:-1]]
    for l in range(DEPTH):
        mod = c_act @ ada_w[l] + ada_b[l]
        shift, scale, gate = jnp.split(mod, 3, axis=-1)
        h = rms_norm(x, norm_w[l]) * (1.0 + scale[:, None, :]) + shift[:, None, :]
        proj = h @ w_in[l]
        (hq, hf, hi, hz, glu_in, cz, sq, sk, sv, sz, gate_logits) = jnp.split(proj, split_at, axis=-1)
        y_a = hgrn2_branch(hq, hf, hi, hz, lower_bounds[l], hgrn_norm_w[l])
        y_b = conformer_conv_branch(glu_in, cz, conv_w[l], conv_b[l], conv_ln_w[l], conv_ln_b[l])
        y_c = stick_breaking_branch(sq, sk, sv, sz)
        ys = jnp.stack([y_a, y_b, y_c.astype(y_a.dtype)], axis=0)
        branches = jnp.einsum('nbsw,nwd->nbsd', ys, w_branch[l])
        gates = jax.nn.sigmoid(gate_logits.astype(jnp.float32)).reshape(B, S, N_BRANCH, D).astype(x.dtype)
        merged = jnp.einsum('bsnd,nbsd->bsd', gates, branches)
        out = merged @ w_out[l]
        x = x + gate[:, None, :] * out
    return rms_norm(x, final_norm_w)
```

```python
import numpy as np
from contextlib import ExitStack, contextmanager
import concourse.bass as bass
import concourse.mybir as mybir
from concourse.bass_utils import run_bass_kernel_spmd

F32 = mybir.dt.float32
BF16 = mybir.dt.bfloat16
AF = mybir.ActivationFunctionType
ALU = mybir.AluOpType

S = 2048
D = 1024
W = 512
NCOL = 8704
SEM_LIMIT = 12000
LV = 173
NV = 2 * LV + 16
NCB = 4 * 128 + 64 + 4 * 512


class SemObj:
    __slots__ = ("sem", "count", "name")

    def __init__(self, sem, name):
        self.sem = sem
        self.count = 0
        self.name = name


class Engine:
    def __init__(self, fw, name, h, is_pe=False, compute=True):
        self.fw = fw
        self.name = name
        self.h = h
        self.is_pe = is_pe
        self.own = []
        self.cur = None
        self.seen = {}
        self.n_wait = 0
        self.n_ins = 0
        if compute:
            self._new_sem()

    def _new_sem(self):
        s = self.fw.new_sem(f"{self.name}_s{len(self.own)}")
        self.own.append(s)
        self.cur = s


class Tile:
    __slots__ = ("ap", "name", "last_write", "reads")

    def __init__(self, ap, name=""):
        self.ap = ap
        self.name = name
        self.last_write = None
        self.reads = {}


class FW:
    def __init__(self, nc, stack):
        self.nc = nc
        self.stack = stack
        self.stacks = [stack]
        self.scope_tiles = [[]]
        self.n_sems = 0
        self.pe = Engine(self, "pe", nc.tensor, is_pe=True)
        self.act = Engine(self, "act", nc.scalar)
        self.dve = Engine(self, "dve", nc.vector)
        self.pool = Engine(self, "pool", nc.gpsimd)
        self.sp = Engine(self, "sp", nc.sync, compute=False)
        self.engines = [self.pe, self.act, self.dve, self.pool, self.sp]

    def new_sem(self, name):
        self.n_sems += 1
        return SemObj(self.stack.enter_context(self.nc.semaphore(name)), name)

    def sbt(self, name, shape, dtype):
        self.n_sb = getattr(self, "n_sb", 0) + 1
        t = self.stacks[-1].enter_context(self.nc.sbuf_tensor(f"sb{self.n_sb}_{name}", list(shape), dtype))
        tl = Tile(t[:], name)
        self.scope_tiles[-1].append(tl)
        return tl

    def sub(self, ap, name=""):
        tl = Tile(ap, name)
        self.scope_tiles[-1].append(tl)
        return tl

    @contextmanager
    def scope(self):
        st = ExitStack()
        self.stacks.append(st)
        self.scope_tiles.append([])
        try:
            yield
        finally:
            tiles = self.scope_tiles.pop()
            self.barrier(tiles)
            self.stacks.pop()
            st.close()

    def barrier(self, tiles):
        for eng in self.engines:
            self._deps(eng, tiles, tiles)

    def _deps(self, eng, reads, writes):
        deps = {}

        def add(s, v, kind):
            if s in eng.own:
                if eng.is_pe or kind != "raw":
                    return
            if deps.get(s, 0) < v:
                deps[s] = v

        for t in reads:
            if t.last_write is not None:
                add(t.last_write[0], t.last_write[1], "raw")
        for t in writes:
            if t.last_write is not None:
                add(t.last_write[0], t.last_write[1], "waw")
            for s, v in t.reads.items():
                add(s, v, "war")
        for s, v in deps.items():
            if eng.seen.get(s, 0) >= v:
                continue
            eng.h.wait_ge(s.sem, v)
            eng.seen[s] = v
            eng.n_wait += 1

    def _mark(self, mark, reads, writes):
        for t in reads:
            if t.reads.get(mark[0], 0) < mark[1]:
                t.reads[mark[0]] = mark[1]
        for t in writes:
            t.last_write = mark
            t.reads = {}

    def op(self, eng, fn, reads=(), writes=()):
        self._deps(eng, reads, writes)
        if eng.cur.count >= SEM_LIMIT:
            eng._new_sem()
        ins = fn(eng.h)
        ins.then_inc(eng.cur.sem, 1)
        eng.cur.count += 1
        eng.n_ins += 1
        self._mark((eng.cur, eng.cur.count), reads, writes)
        return ins

    def dma(self, q, semobj, pairs, reads=(), writes=()):
        self._deps(q, reads, writes)
        for out, in_ in pairs:
            q.h.dma_start(out=out, in_=in_).then_inc(semobj.sem, 16)
            semobj.count += 16
            q.n_ins += 1
        self._mark((semobj, semobj.count), reads, writes)


def build(n_layers=2, final_norm=True, debug=False):
    nc = bass.Bass("TRN2", target_bir_lowering=False)
    x_d = nc.dram_tensor("x", [S, D], F32, kind="ExternalInput").ap()
    vecs_d = nc.dram_tensor("vecs", [128, NV], F32, kind="ExternalInput").ap()
    identf_d = nc.dram_tensor("identf", [128, 128], F32, kind="ExternalInput").ap()
    constb_d = nc.dram_tensor("constb", [128, NCB], F32, kind="ExternalInput").ap()
    fnw_d = nc.dram_tensor("fnw", [128, D], F32, kind="ExternalInput").ap()
    ada_w_d = nc.dram_tensor("ada_w", [2, D, 3 * D], F32, kind="ExternalInput").ap()
    w_in_d = nc.dram_tensor("w_in", [2, D, NCOL], F32, kind="ExternalInput").ap()
    w_br_d = nc.dram_tensor("w_branch", [2, 3, W, D], F32, kind="ExternalInput").ap()
    w_out_d = nc.dram_tensor("w_out", [2, D, D], F32, kind="ExternalInput").ap()
    out_d = nc.dram_tensor("out", [S, D], F32, kind="ExternalOutput").ap()
    if debug:
        dbg_h = nc.dram_tensor("dbg_h", [128, 8, S], F32, kind="ExternalOutput").ap()
        dbg_y = nc.dram_tensor("dbg_y", [3, 128, 4, S], F32, kind="ExternalOutput").ap()

    with ExitStack() as st:
        fw = FW(nc, st)
        pe, act, dve, pool, sp = fw.pe, fw.act, fw.dve, fw.pool, fw.sp

        XRf = st.enter_context(nc.sbuf_tensor("x_res", [128, 16, D], F32))
        XR = [fw.sub(XRf[:, i, :], f"xr{i}") for i in range(16)]
        HTf = st.enter_context(nc.sbuf_tensor("hT", [128, 8, S], BF16))
        HT = [fw.sub(HTf[:, :, tt * 512:(tt + 1) * 512], f"ht{tt}") for tt in range(4)]
        YTf = [st.enter_context(nc.sbuf_tensor(f"yT{n}", [128, 4, S], BF16)) for n in range(3)]
        YT = [[fw.sub(YTf[n][:, :, tt * 512:(tt + 1) * 512], f"yt{n}_{tt}") for tt in range(4)] for n in range(3)]
        WS = [fw.sbt(f"wslot{i}", [128, 4608], BF16) for i in range(2)]
        WSEM = [fw.new_sem(f"wsem{i}") for i in range(2)]
        vecs = fw.sbt("vecs", [128, NV], F32)
        identf = fw.sbt("identf", [128, 128], F32)
        onesf = fw.sbt("onesf", [128, 128], F32)
        constb = fw.sbt("constb", [128, NCB], BF16)
        identb = constb.ap[:, 0:128]
        trineg = constb.ap[:, 128:256]
        onesneg = constb.ap[:, 256:384]
        onesb = constb.ap[:, 384:512]
        maskH = constb.ap[:, 512:576]
        maskA = [constb.ap[:, 576 + dk * 512:576 + (dk + 1) * 512] for dk in range(4)]
        small = fw.sbt("small", [128, 256], F32)
        ss = small.ap[:, 0:16]
        rstd = small.ap[:, 16:32]
        mhalf = small.ap[:, 32:48]
        cact = small.ap[:, 48:64]
        modT = [small.ap[:, 64 + l * 24:64 + (l + 1) * 24] for l in range(2)]
        Avec = [small.ap[:, 112 + l * 8:112 + (l + 1) * 8] for l in range(2)]
        c1v = [small.ap[:, 128 + l * 4:128 + (l + 1) * 4] for l in range(2)]
        c0v = [small.ap[:, 136 + l * 4:136 + (l + 1) * 4] for l in range(2)]
        sm_tmp = small.ap[:, 144:176]
        cwhT = fw.sbt("cwh", [128, 2 * 124], F32)
        psall = st.enter_context(nc.psum_tensor("psall", [128, 8 * 512], F32))
        B = [fw.sub(psall[:, i * 512:(i + 1) * 512], f"bank{i}") for i in range(8)]
        XSEM = [fw.new_sem(f"xsem{i}") for i in range(16)]
        csem = [fw.new_sem(f"csem{i}") for i in range(4)]

        def vcol(l, off, n=1):
            return vecs.ap[:, l * LV + off:l * LV + off + n]

        fw.dma(sp, csem[0], [(vecs.ap, vecs_d)], writes=[vecs])
        fw.dma(sp, csem[1], [(identf.ap, identf_d)], writes=[identf])
        fw.dma(pool, csem[2], [(constb.ap[:, 0:1312], constb_d[:, 0:1312]), (constb.ap[:, 1312:NCB], constb_d[:, 1312:NCB])], writes=[constb])
        x_loads = [(lambda i=i: fw.dma(sp, XSEM[i], [(XR[i].ap, x_d[i * 128:(i + 1) * 128, :])], writes=[XR[i]])) for i in range(16)]
        fw.op(pool, lambda e: e.memset(onesf.ap, 1.0), writes=[onesf])
        fw.op(pool, lambda e: e.memset(mhalf, -0.5), writes=[small])

        jobs = []
        for l in range(n_layers):
            for hd in range(4):
                jobs.append(("hgrn", l, hd))
            for cb in range(4):
                jobs.append(("conv", l, cb))
            jobs.append(("convz", l, 0))
            for hp in range(4):
                jobs.append(("attn", l, hp))
            for half in range(2):
                for j in range(8):
                    jobs.append(("merge", l, j))
                for cc in range(2):
                    jobs.append(("outp", l, cc))
        issued = [0]

        def wsrc(l, c0, n):
            return w_in_d[l, :, c0:c0 + n].rearrange("(kb p) n -> p kb n", p=128)

        def issue(k):
            kind, l, idx = jobs[k]
            slot = WS[k % 2]
            v512 = slot.ap[:, 0:4096].rearrange("p (kb n) -> p kb n", n=512)
            pairs = []
            if kind == "hgrn":
                for b in range(4):
                    pairs.append((v512[:, :, b * 128:(b + 1) * 128], wsrc(l, b * 512 + idx * 128, 128)))
            elif kind == "conv":
                pairs.append((v512[:, :, 0:128], wsrc(l, 2048 + idx * 128, 128)))
                pairs.append((v512[:, :, 128:256], wsrc(l, 2560 + idx * 128, 128)))
            elif kind == "convz":
                pairs.append((v512, wsrc(l, 3072, 512)))
            elif kind == "attn":
                for b in range(4):
                    pairs.append((v512[:, :, b * 128:(b + 1) * 128], wsrc(l, 3584 + b * 512 + idx * 128, 128)))
            elif kind == "merge":
                vg = slot.ap[:, 0:3072].rearrange("p (kb n) -> p kb n", n=384)
                vb = slot.ap[:, 3072:4608].rearrange("p (n wb d) -> p n wb d", n=3, wb=4)
                for n in range(3):
                    pairs.append((vg[:, :, n * 128:(n + 1) * 128], wsrc(l, 5632 + n * 1024 + idx * 128, 128)))
                    pairs.append((vb[:, n, :, :],
                                  w_br_d[l, n, :, idx * 128:(idx + 1) * 128].rearrange("(wb p) d -> p wb d", p=128)))
            elif kind == "outp":
                pairs.append((v512, w_out_d[l, :, idx * 512:(idx + 1) * 512].rearrange("(kb p) n -> p kb n", p=128)))
            fw.dma(pool, WSEM[k % 2], pairs, writes=[slot])

        def acquire(expect):
            k = acquire.k
            assert jobs[k][0] == expect, (jobs[k], expect)
            while issued[0] <= min(k + 1, len(jobs) - 1):
                issue(issued[0])
                issued[0] += 1
            acquire.k += 1
            return WS[k % 2]

        acquire.k = 0

        cT2 = vecs.ap[:, 2 * LV:2 * LV + 16]
        fw.op(act, lambda e: e.activation(out=cact, in_=cT2, func=AF.Silu), reads=[vecs], writes=[small])
        def mod_steps(l, AW, awsem, bank):
            cact3 = cact.rearrange("p (k t) -> p k t", t=2)
            steps = []
            for jg in range(6):
                aw = AW[jg % 2]

                def ld(jg=jg, aw=aw):
                    fw.dma(sp, awsem[jg % 2],
                           [(aw.ap, ada_w_d[l, :, jg * 512:(jg + 1) * 512].rearrange("(kb p) n -> p kb n", p=128))],
                           writes=[aw])
                steps.append(ld)
                for jj in range(4):
                    def mm(jg=jg, jj=jj, aw=aw):
                        col = 2 * (jg * 4 + jj)
                        for kb in range(8):
                            fw.op(pe, lambda e: e.matmul(bank.ap[:, col:col + 2], lhsT=aw.ap[:, kb, jj * 128:(jj + 1) * 128],
                                                         rhs=cact3[:, kb, :], start=(kb == 0), stop=(kb == 7)),
                                  reads=[aw, small], writes=[bank])
                    steps.append(mm)
            return steps

        def mod_finish(l, bank):
            src = bank.ap[:, 0:48].rearrange("p (j t) -> p j t", t=2)[:, :, 0]
            fw.op(dve, lambda e: e.tensor_tensor(out=modT[l], in0=src, in1=vcol(l, 0, 24), op=ALU.add),
                  reads=[bank, vecs], writes=[small])
            fw.op(dve, lambda e: e.scalar_tensor_tensor(out=Avec[l], in0=modT[l][:, 8:16], scalar=1.0, in1=vcol(l, 24, 8),
                                                        op0=ALU.add, op1=ALU.mult), reads=[small, vecs], writes=[small])

        with fw.scope():
            AW = [fw.sbt(f"aw{i}", [128, 8, 512], F32) for i in range(2)]
            awsem = [fw.new_sem(f"awsem{i}") for i in range(2)]
            n_ld = 0
            for f_ in mod_steps(0, AW, awsem, B[0]):
                f_()
                if getattr(f_, "__name__", "") == "ld":
                    n_ld += 1
                    if n_ld >= 2:
                        for _ in range(4):
                            if x_loads:
                                x_loads.pop(0)()
            while x_loads:
                x_loads.pop(0)()
            mod_finish(0, B[0])
            a0 = vcol(0, 32, 4)
            a1 = vcol(1, 32, 4)
            t_mx, t_e0, t_e1, t_s, t_p0, t_p1, t_lo = [sm_tmp[:, 4 * i:4 * i + 4] for i in range(7)]
            fw.op(dve, lambda e: e.tensor_tensor(out=t_mx, in0=a0, in1=a1, op=ALU.max), reads=[vecs], writes=[small])
            fw.op(dve, lambda e: e.tensor_tensor(out=t_e0, in0=a0, in1=t_mx, op=ALU.subtract), reads=[vecs, small], writes=[small])
            fw.op(dve, lambda e: e.tensor_tensor(out=t_e1, in0=a1, in1=t_mx, op=ALU.subtract), reads=[vecs, small], writes=[small])
            fw.op(act, lambda e: e.activation(out=t_e0, in_=t_e0, func=AF.Exp), reads=[small], writes=[small])
            fw.op(act, lambda e: e.activation(out=t_e1, in_=t_e1, func=AF.Exp), reads=[small], writes=[small])
            fw.op(dve, lambda e: e.tensor_tensor(out=t_s, in0=t_e0, in1=t_e1, op=ALU.add), reads=[small], writes=[small])
            fw.op(dve, lambda e: e.reciprocal(out=t_s, in_=t_s), reads=[small], writes=[small])
            fw.op(dve, lambda e: e.tensor_tensor(out=t_p0, in0=t_e0, in1=t_s, op=ALU.mult), reads=[small], writes=[small])
            fw.op(dve, lambda e: e.tensor_tensor(out=t_p1, in0=t_e1, in1=t_s, op=ALU.mult), reads=[small], writes=[small])
            for l in range(2):
                if l == 0:
                    fw.op(dve, lambda e: e.tensor_tensor(out=t_lo, in0=t_p0, in1=t_p0, op=ALU.subtract), reads=[small], writes=[small])
                else:
                    fw.op(dve, lambda e: e.tensor_tensor(out=t_lo, in0=t_p0, in1=t_p1, op=ALU.add), reads=[small], writes=[small])
                    fw.op(dve, lambda e: e.tensor_tensor(out=t_lo, in0=t_lo, in1=t_p0, op=ALU.subtract), reads=[small], writes=[small])
                fw.op(dve, lambda e: e.tensor_scalar(out=c1v[l], in0=t_lo, scalar1=-0.5, scalar2=0.5, op0=ALU.mult, op1=ALU.add),
                      reads=[small], writes=[small])
                fw.op(dve, lambda e: e.tensor_scalar(out=c0v[l], in0=t_lo, scalar1=0.5, scalar2=0.5, op0=ALU.mult, op1=ALU.add),
                      reads=[small], writes=[small])
            for l in range(2):
                fw.op(dve, lambda e: e.tensor_scalar(out=cwhT.ap[:, l * 124:(l + 1) * 124], in0=vcol(l, 37, 124), scalar1=0.5,
                                                     scalar2=None, op0=ALU.mult), reads=[vecs], writes=[cwhT])

        for l in range(n_layers):
            with fw.scope():
                junk = fw.sbt("junk", [128, D], BF16)
                dg = [fw.sbt(f"dg{i}", [128, 128], F32) for i in range(2)]
                msteps = []
                if l + 1 < n_layers:
                    AW2 = [fw.sbt(f"aw2{i}", [128, 8, 512], F32) for i in range(2)]
                    awsem2 = [fw.new_sem(f"awsem2_{l}_{i}") for i in range(2)]
                    msteps = mod_steps(l + 1, AW2, awsem2, B[4])
                mi = 0
                for i in range(16):
                    fw.op(act, lambda e: e.activation(out=junk.ap, in_=XR[i].ap, func=AF.Square, accum_out=ss[:, i:i + 1]),
                          reads=[XR[i]], writes=[junk, small])
                fw.op(dve, lambda e: e.tensor_scalar(out=ss, in0=ss, scalar1=1.0 / D, scalar2=1e-6, op0=ALU.mult, op1=ALU.add),
                      reads=[small], writes=[small])
                fw.op(pool, lambda e: e.tensor_tensor(out=rstd, in0=ss, in1=mhalf, op=ALU.pow), reads=[small], writes=[small])
                for i in range(16):
                    d_ = dg[i % 2]
                    fw.op(dve, lambda e: e.tensor_scalar(out=d_.ap, in0=identf.ap, scalar1=rstd[:, i:i + 1], scalar2=None, op0=ALU.mult),
                          reads=[identf, small], writes=[d_])
                    for jh in range(2):
                        bk = B[(2 * i + jh) % 4]
                        for jj in range(4):
                            j = jh * 4 + jj
                            fw.op(pe, lambda e: e.matmul(bk.ap[:, jj * 128:(jj + 1) * 128], lhsT=XR[i].ap[:, j * 128:(j + 1) * 128],
                                                         rhs=d_.ap, start=True, stop=True), reads=[XR[i], d_], writes=[bk])
                        for jj in range(4):
                            j = jh * 4 + jj
                            o_ap = HTf[:, j, i * 128:(i + 1) * 128]
                            i_ap = bk.ap[:, jj * 128:(jj + 1) * 128]
                            if jj % 2 == 0:
                                fw.op(act, lambda e: e.activation(out=o_ap, in_=i_ap, func=AF.Identity, scale=Avec[l][:, j:j + 1],
                                                                  bias=modT[l][:, j:j + 1]), reads=[bk, small], writes=[HT[i // 4]])
                            else:
                                fw.op(dve, lambda e: e.tensor_scalar(out=o_ap, in0=i_ap, scalar1=Avec[l][:, j:j + 1],
                                                                     scalar2=modT[l][:, j:j + 1], op0=ALU.mult, op1=ALU.add),
                                      reads=[bk, small], writes=[HT[i // 4]])
                    for _ in range(2):
                        if mi < len(msteps):
                            msteps[mi]()
                            mi += 1
                while mi < len(msteps):
                    msteps[mi]()
                    mi += 1
                if l + 1 < n_layers:
                    mod_finish(l + 1, B[4])

            with fw.scope():
                qf = fw.sbt("h_qf", [128, 512], F32)
                tf = fw.sbt("h_tf", [128, 512], F32)
                kf = fw.sbt("h_kf", [128, 512], F32)
                bb = fw.sbt("h_bb", [128, 512], F32)
                e1 = fw.sbt("h_e1", [128, 512], F32)
                e2 = fw.sbt("h_e2", [128, 512], F32)
                r1 = fw.sbt("h_r1", [128, 512], F32)
                r2 = fw.sbt("h_r2", [128, 512], F32)
                o2 = fw.sbt("h_o2", [128, 512], BF16)
                ones64 = fw.sbt("h_ones64", [128, 64], F32)
                PB = []
                for i in range(2):
                    PB.append(dict(
                        zs=fw.sbt(f"h_zs{i}", [128, 512], BF16), qtl=fw.sbt(f"h_qtl{i}", [128, 512], BF16),
                        ktl=fw.sbt(f"h_ktl{i}", [128, 512], BF16), ktok=fw.sbt(f"h_ktok{i}", [128, 4, 128], BF16),
                        vtok=fw.sbt(f"h_vtok{i}", [128, 4, 128], BF16), hs=fw.sbt(f"h_small{i}", [128, 56], F32)))
                attnT = [fw.sbt(f"h_attnT{i}", [128, 64], BF16) for i in range(2)]
                aTc = [fw.sbt(f"h_aTc{i}", [128, 64], F32) for i in range(2)]
                Sst = [fw.sbt(f"h_S{i}", [128, 128], F32) for i in range(2)]
                Sb = [fw.sbt(f"h_Sb{i}", [128, 128], BF16) for i in range(2)]
                tmpU = [fw.sbt(f"h_tmpU{i}", [128, 128], F32) for i in range(2)]
                Up = [fw.sub(B[6].ap[:, c * 128:(c + 1) * 128], f"up{c}") for c in range(2)]
                ATp = [fw.sub(B[6].ap[:, 256 + c * 64:256 + (c + 1) * 64], f"atp{c}") for c in range(4)]
                fw.barrier([B[6]])
                fw.op(pool, lambda e: e.memset(ones64.ap, 1.0), writes=[ones64])
                nw = vcol(l, 36, 1)
                wcur = {}

                def prep_steps(g):
                    hd, tt = divmod(g, 4)
                    P = PB[g % 2]
                    zs, qtl, ktl, ktok, vtok, hs = P["zs"], P["qtl"], P["ktl"], P["ktok"], P["vtok"], P["hs"]
                    bm, bl, dlm = hs.ap[:, 0:8], hs.ap[:, 8:16], hs.ap[:, 16:24]
                    h_ = HT[tt]
                    st_ = []

                    def s_acq():
                        if tt == 0:
                            wcur[hd] = acquire("hgrn")
                    st_.append(s_acq)

                    def mm_block(bk, c0):
                        def f():
                            wsl = wcur[hd]
                            Wh = wsl.ap[:, 0:4096].rearrange("p (kb n) -> p kb n", n=512)
                            for kb in range(8):
                                fw.op(pe, lambda e: e.matmul(bk.ap, lhsT=Wh[:, kb, c0:c0 + 128], rhs=h_.ap[:, kb, :],
                                                             start=(kb == 0), stop=(kb == 7)), reads=[wsl, h_], writes=[bk])
                        return f
                    def mm_v(ib):
                        def f():
                            wsl = wcur[hd]
                            Wh = wsl.ap[:, 0:4096].rearrange("p (kb n) -> p kb n", n=512)
                            for kb in range(8):
                                fw.op(pe, lambda e: e.matmul(B[3].ap[:, ib * 128:(ib + 1) * 128], lhsT=h_.ap[:, kb, ib * 128:(ib + 1) * 128],
                                                             rhs=Wh[:, kb, 256:384], start=(kb == 0), stop=(kb == 7)),
                                      reads=[wsl, h_], writes=[B[3]])
                        return f

                    def scans(c0_):
                        def f():
                            for c in range(c0_, c0_ + 4):
                                fw.op(dve, lambda e: e.tensor_tensor_scan(out=bb.ap[:, c * 64:(c + 1) * 64], data0=ones64.ap,
                                                                          data1=e1.ap[:, c * 64:(c + 1) * 64], initial=0.0,
                                                                          op0=ALU.mult, op1=ALU.add), reads=[ones64, e1], writes=[bb])
                        return f
                    bb3 = bb.ap.rearrange("p (c l) -> p c l", l=64)

                    def smalls():
                        fw.op(dve, lambda e: e.tensor_copy(out=bm, in_=bb3[:, :, 31]), reads=[bb], writes=[hs])
                        fw.op(dve, lambda e: e.tensor_copy(out=bl, in_=bb3[:, :, 63]), reads=[bb], writes=[hs])
                        fw.op(dve, lambda e: e.tensor_tensor(out=dlm, in0=bl, in1=bm, op=ALU.subtract), reads=[hs], writes=[hs])

                    def kappa():
                        hprev = PB[(g + 1) % 2]["hs"]
                        fw.op(dve, lambda e: e.tensor_tensor(out=hs.ap[:, 49:56], in0=hs.ap[:, 25:32], in1=hs.ap[:, 40:47], op=ALU.mult),
                              reads=[hs], writes=[hs])
                        if tt == 0:
                            fw.op(dve, lambda e: e.tensor_copy(out=hs.ap[:, 48:49], in_=hs.ap[:, 24:25]), reads=[hs], writes=[hs])
                        else:
                            fw.op(dve, lambda e: e.tensor_tensor(out=hs.ap[:, 48:49], in0=hs.ap[:, 24:25], in1=hprev.ap[:, 47:48], op=ALU.mult),
                                  reads=[hs, hprev], writes=[hs])

                    st_.append(mm_block(B[1], 128))
                    st_.append(lambda: fw.op(act, lambda e: e.activation(out=tf.ap, in_=B[1].ap, func=AF.Tanh, scale=0.5), reads=[B[1]], writes=[tf]))
                    st_.append(lambda: fw.op(dve, lambda e: e.tensor_scalar(out=tf.ap, in0=tf.ap, scalar1=c1v[l][:, hd:hd + 1],
                                                                            scalar2=c0v[l][:, hd:hd + 1], op0=ALU.mult, op1=ALU.add),
                                             reads=[tf, small], writes=[tf]))
                    st_.append(mm_block(B[0], 0))
                    st_.append(lambda: fw.op(act, lambda e: e.activation(out=qf.ap, in_=B[0].ap, func=AF.Silu), reads=[B[0]], writes=[qf]))
                    st_.append(lambda: fw.op(act, lambda e: e.activation(out=kf.ap, in_=tf.ap, func=AF.Identity, scale=-1.0, bias=1.0),
                                             reads=[tf], writes=[kf]))
                    st_.append(lambda: fw.op(dve, lambda e: e.tensor_scalar_max(out=bb.ap, in0=tf.ap, scalar1=1e-30), reads=[tf], writes=[bb]))
                    st_.append(mm_block(B[2], 384))
                    st_.append(lambda: fw.op(act, lambda e: e.activation(out=zs.ap, in_=B[2].ap, func=AF.Silu), reads=[B[2]], writes=[zs]))
                    st_.append(lambda: fw.op(act, lambda e: e.activation(out=e1.ap, in_=bb.ap, func=AF.Ln), reads=[bb], writes=[e1]))
                    st_.append(mm_v(0))
                    st_.append(scans(0))
                    st_.append(mm_v(1))
                    st_.append(scans(4))
                    st_.append(mm_v(2))
                    st_.append(smalls)
                    st_.append(mm_v(3))
                    st_.append(lambda: fw.op(dve, lambda e: e.tensor_tensor(out=tf.ap.rearrange("p (c l) -> p c l", l=64), in0=bb3,
                                                                            in1=bm.unsqueeze(2).to_broadcast([128, 8, 64]), op=ALU.subtract),
                                             reads=[bb, hs], writes=[tf]))
                    st_.append(lambda: fw.op(dve, lambda e: e.tensor_copy(out=vtok.ap, in_=B[3].ap.rearrange("p (i n) -> p i n", n=128)),
                                             reads=[B[3]], writes=[vtok]))
                    st_.append(lambda: fw.op(dve, lambda e: e.tensor_scalar(out=tf.ap, in0=tf.ap, scalar1=80.0, scalar2=-80.0, op0=ALU.min, op1=ALU.max),
                                             reads=[tf], writes=[tf]))
                    st_.append(lambda: fw.op(act, lambda e: e.activation(out=e1.ap, in_=tf.ap, func=AF.Exp), reads=[tf], writes=[e1]))
                    st_.append(lambda: fw.op(act, lambda e: e.activation(out=e2.ap, in_=tf.ap, func=AF.Exp, scale=-1.0), reads=[tf], writes=[e2]))
                    st_.append(lambda: fw.op(act, lambda e: e.activation(out=hs.ap[:, 24:48], in_=hs.ap[:, 0:24], func=AF.Exp), reads=[hs], writes=[hs]))
                    st_.append(kappa)
                    st_.append(lambda: fw.op(dve, lambda e: e.scalar_tensor_tensor(out=qtl.ap, in0=qf.ap, scalar=128.0 ** -0.5, in1=e1.ap,
                                                                                   op0=ALU.mult, op1=ALU.mult), reads=[qf, e1], writes=[qtl]))
                    st_.append(lambda: fw.op(dve, lambda e: e.tensor_tensor(out=ktl.ap, in0=kf.ap, in1=e2.ap, op=ALU.mult), reads=[kf, e2], writes=[ktl]))

                    def transposes():
                        pb7 = B[7].ap.bitcast(BF16)
                        for ib in range(4):
                            fw.op(pe, lambda e: e.transpose(out=pb7[:, ib * 128:(ib + 1) * 128], in_=ktl.ap[:, ib * 128:(ib + 1) * 128],
                                                            identity=identb), reads=[ktl, constb], writes=[B[7]])
                        fw.op(act, lambda e: e.activation(out=ktok.ap, in_=pb7[:, 0:512].rearrange("p (i n) -> p i n", n=128),
                                                          func=AF.Identity), reads=[B[7]], writes=[ktok])
                    st_.append(transposes)
                    return st_

                def recur(g, steps):
                    hd, tt = divmod(g, 4)
                    P = PB[g % 2]
                    zs, qtl, ktl, ktok, vtok, hs = P["zs"], P["qtl"], P["ktl"], P["ktok"], P["vtok"], P["hs"]
                    kap = hs.ap[:, 48:56]
                    OTb = B[4 + g % 2]
                    per = -(-len(steps) // 8) if steps else 0
                    si = 0
                    if tt == 0:
                        fw.op(pool, lambda e: e.memset(Sst[0].ap, 0.0), writes=[Sst[0]])
                    fw.op(act, lambda e: e.activation(out=Sb[0].ap, in_=Sst[0].ap, func=AF.Identity, scale=kap[:, 0:1]),
                          reads=[Sst[0], hs], writes=[Sb[0]])
                    for c in range(8):
                        pr = (c % 2) * 64
                        ib = c // 2
                        cs = slice(c * 64, (c + 1) * 64)
                        aT, ac, up, tu = attnT[c % 2], aTc[c % 2], Up[c % 2], tmpU[c % 2]
                        s_old, s_new = Sst[c % 2], Sst[(c + 1) % 2]
                        atp = ATp[c % 4]
                        fw.op(pe, lambda e: e.matmul(atp.ap[pr:pr + 64, :], lhsT=ktl.ap[:, cs], rhs=qtl.ap[:, cs], start=True, stop=True),
                              reads=[ktl, qtl], writes=[atp])
                        fw.op(pe, lambda e: e.matmul(up.ap, lhsT=ktok.ap[pr:pr + 64, ib, :], rhs=vtok.ap[pr:pr + 64, ib, :],
                                                     start=True, stop=True), reads=[ktok, vtok], writes=[up])
                        fw.op(dve, lambda e: e.tensor_scalar(out=ac.ap[pr:pr + 64, :], in0=atp.ap[pr:pr + 64, :], scalar1=1e30, scalar2=-1e30,
                                                             op0=ALU.min, op1=ALU.max), reads=[atp], writes=[ac])
                        fw.op(dve, lambda e: e.tensor_tensor(out=aT.ap[pr:pr + 64, :], in0=ac.ap[pr:pr + 64, :], in1=maskH[pr:pr + 64, :],
                                                             op=ALU.mult), reads=[ac, constb], writes=[aT])
                        fw.op(dve, lambda e: e.scalar_tensor_tensor(out=s_new.ap, in0=s_old.ap, scalar=kap[:, c:c + 1], in1=up.ap,
                                                                    op0=ALU.mult, op1=ALU.add), reads=[s_old, hs, up], writes=[s_new])
                        if c < 7:
                            fw.op(act, lambda e: e.activation(out=Sb[(c + 1) % 2].ap, in_=s_new.ap, func=AF.Identity, scale=kap[:, c + 1:c + 2]),
                                  reads=[s_new, hs], writes=[Sb[(c + 1) % 2]])
                        for _ in range(per):
                            if si < len(steps):
                                steps[si]()
                                si += 1
                        fw.op(pe, lambda e: e.matmul(OTb.ap[:, cs], lhsT=vtok.ap[pr:pr + 64, ib, :], rhs=aT.ap[pr:pr + 64, :],
                                                     start=True, stop=False), reads=[vtok, aT], writes=[OTb])
                        fw.op(pe, lambda e: e.matmul(OTb.ap[:, cs], lhsT=Sb[c % 2].ap, rhs=qtl.ap[:, cs], start=False, stop=True),
                              reads=[Sb[c % 2], qtl], writes=[OTb])
                    while si < len(steps):
                        steps[si]()
                        si += 1
                    fw.op(act, lambda e: e.activation(out=o2.ap, in_=OTb.ap, func=AF.Square), reads=[OTb], writes=[o2])
                    fw.op(pe, lambda e: e.matmul(B[7].ap, lhsT=onesb, rhs=o2.ap, start=True, stop=True), reads=[constb, o2], writes=[B[7]])
                    fw.op(act, lambda e: e.activation(out=r1.ap, in_=B[7].ap, func=AF.Ln, scale=1.0 / 128, bias=1e-6), reads=[B[7]], writes=[r1])
                    fw.op(act, lambda e: e.activation(out=r1.ap, in_=r1.ap, func=AF.Exp, scale=-0.5), reads=[r1], writes=[r1])
                    fw.op(dve, lambda e: e.tensor_tensor(out=r2.ap, in0=OTb.ap, in1=r1.ap, op=ALU.mult), reads=[OTb, r1], writes=[r2])
                    fw.op(dve, lambda e: e.scalar_tensor_tensor(out=YT[0][tt].ap[:, hd, :], in0=r2.ap, scalar=nw, in1=zs.ap,
                                                                op0=ALU.mult, op1=ALU.mult), reads=[r2, vecs, zs], writes=[YT[0][tt]])

                for f_ in prep_steps(0):
                    f_()
                for g in range(16):
                    recur(g, prep_steps(g + 1) if g < 15 else [])

            with fw.scope():
                Ub = [fw.sbt(f"c_U{i}", [128, 30 + S], BF16) for i in range(2)]
                Dg = [fw.sbt(f"c_D{i}", [128, 31, 128], BF16) for i in range(2)]
                tg = [fw.sbt(f"c_tg{i}", [128, 512], F32) for i in range(2)]
                for i in range(2):
                    fw.op(pool, lambda e: e.memset(Ub[i].ap[:, 0:30], 0.0), writes=[Ub[i]])
                for cb in range(4):
                    wsl = acquire("conv")
                    Wc = wsl.ap[:, 0:4096].rearrange("p (kb n) -> p kb n", n=512)
                    ub = Ub[cb % 2]
                    dgc = Dg[cb % 2]
                    for k in range(31):
                        fw.op(dve, lambda e: e.tensor_scalar(out=dgc.ap[:, k, :], in0=identb,
                                                              scalar1=cwhT.ap[:, l * 124 + cb * 31 + k:l * 124 + cb * 31 + k + 1],
                                                              scalar2=None, op0=ALU.mult), reads=[constb, cwhT], writes=[dgc])
                    for tt in range(4):
                        h_ = HT[tt]
                        ba, bg = B[tt % 2], B[2 + tt % 2]
                        for (bk, c0) in ((ba, 0), (bg, 128)):
                            for kb in range(8):
                                fw.op(pe, lambda e: e.matmul(bk.ap, lhsT=Wc[:, kb, c0:c0 + 128], rhs=h_.ap[:, kb, :],
                                                             start=(kb == 0), stop=(kb == 7)), reads=[wsl, h_], writes=[bk])
                        t_ = tg[tt % 2]
                        fw.op(act, lambda e: e.activation(out=t_.ap, in_=bg.ap, func=AF.Tanh, scale=0.5), reads=[bg], writes=[t_])
                        fw.op(dve, lambda e: e.scalar_tensor_tensor(out=ub.ap[:, 30 + tt * 512:30 + (tt + 1) * 512], in0=t_.ap, scalar=1.0,
                                                                    in1=ba.ap, op0=ALU.add, op1=ALU.mult), reads=[t_, ba], writes=[ub])
                    for tt in range(4):
                        by = B[4 + tt % 2]
                        for k in range(31):
                            fw.op(pe, lambda e: e.matmul(by.ap, lhsT=dgc.ap[:, k, :], rhs=ub.ap[:, tt * 512 + k:tt * 512 + k + 512],
                                                         start=(k == 0), stop=(k == 30)), reads=[dgc, ub], writes=[by])
                        fw.op(dve, lambda e: e.tensor_scalar(out=YT[1][tt].ap[:, cb, :], in0=by.ap, scalar1=vcol(l, 161 + cb, 1), scalar2=None,
                                                             op0=ALU.add), reads=[by, vecs], writes=[YT[1][tt]])

            with fw.scope():
                ysq = [fw.sbt(f"c_ysq{i}", [128, 512], BF16) for i in range(4)]
                mean = fw.sbt("c_mean", [128, 512], F32)
                m2 = fw.sbt("c_m2", [128, 512], F32)
                rs = fw.sbt("c_rs", [128, 512], F32)
                zsc = [fw.sbt(f"c_zs{i}", [128, 512], BF16) for i in range(2)]
                dd = [fw.sbt(f"c_dd{i}", [128, 512], F32) for i in range(2)]
                tt_ = [fw.sbt(f"c_tt{i}", [128, 512], BF16) for i in range(2)]
                wsl = acquire("convz")
                Wz = wsl.ap[:, 0:4096].rearrange("p (kb n) -> p kb n", n=512)
                for tt in range(4):
                    y_ = YT[1][tt]
                    h_ = HT[tt]
                    for cb in range(4):
                        fw.op(act, lambda e: e.activation(out=ysq[cb].ap, in_=y_.ap[:, cb, :], func=AF.Square),
                              reads=[y_], writes=[ysq[cb]])
                    for cb in range(4):
                        fw.op(pe, lambda e: e.matmul(B[6].ap, lhsT=onesb, rhs=y_.ap[:, cb, :], start=(cb == 0), stop=(cb == 3)),
                              reads=[constb, y_], writes=[B[6]])
                    for cb in range(4):
                        fw.op(pe, lambda e: e.matmul(B[7].ap, lhsT=onesb, rhs=ysq[cb].ap, start=(cb == 0), stop=(cb == 3)),
                              reads=[constb, ysq[cb]], writes=[B[7]])
                    fw.op(dve, lambda e: e.tensor_scalar(out=mean.ap, in0=B[6].ap, scalar1=1.0 / W, scalar2=None, op0=ALU.mult),
                          reads=[B[6]], writes=[mean])
                    fw.op(dve, lambda e: e.tensor_tensor(out=m2.ap, in0=mean.ap, in1=mean.ap, op=ALU.mult), reads=[mean], writes=[m2])
                    fw.op(dve, lambda e: e.scalar_tensor_tensor(out=rs.ap, in0=B[7].ap, scalar=1.0 / W, in1=m2.ap, op0=ALU.mult, op1=ALU.subtract),
                          reads=[B[7], m2], writes=[rs])
                    fw.op(act, lambda e: e.activation(out=rs.ap, in_=rs.ap, func=AF.Ln, bias=1e-6), reads=[rs], writes=[rs])
                    fw.op(act, lambda e: e.activation(out=rs.ap, in_=rs.ap, func=AF.Exp, scale=-0.5), reads=[rs], writes=[rs])
                    for cb in range(4):
                        bz = B[cb % 2]
                        for kb in range(8):
                            fw.op(pe, lambda e: e.matmul(bz.ap, lhsT=Wz[:, kb, cb * 128:(cb + 1) * 128], rhs=h_.ap[:, kb, :],
                                                         start=(kb == 0), stop=(kb == 7)), reads=[wsl, h_], writes=[bz])
                        z_ = zsc[cb % 2]
                        d_ = dd[cb % 2]
                        t_ = tt_[cb % 2]
                        fw.op(act, lambda e: e.activation(out=z_.ap, in_=bz.ap, func=AF.Silu), reads=[bz], writes=[z_])
                        fw.op(dve, lambda e: e.tensor_tensor(out=d_.ap, in0=y_.ap[:, cb, :], in1=mean.ap, op=ALU.subtract),
                              reads=[y_, mean], writes=[d_])
                        fw.op(dve, lambda e: e.tensor_tensor(out=d_.ap, in0=d_.ap, in1=rs.ap, op=ALU.mult), reads=[d_, rs], writes=[d_])
                        fw.op(act, lambda e: e.activation(out=t_.ap, in_=d_.ap, func=AF.Silu, scale=vcol(l, 165 + cb, 1), bias=vcol(l, 169 + cb, 1)),
                              reads=[d_, vecs], writes=[t_])
                        fw.op(dve, lambda e: e.tensor_tensor(out=y_.ap[:, cb, :], in0=t_.ap, in1=z_.ap, op=ALU.mult),
                              reads=[t_, z_], writes=[y_])

            with fw.scope():
                qTm = [fw.sbt(f"a_qT{i}", [128, S], BF16) for i in range(2)]
                kT = fw.sbt("a_kT", [128, S], BF16)
                zsT = fw.sbt("a_zsT", [128, S], BF16)
                vtk = fw.sbt("a_vtok", [128, 16, 128], BF16)
                Eb = [fw.sbt(f"a_E{i}", [128, 512], F32) for i in range(2)]
                SPb = [fw.sbt(f"a_SP{i}", [128, 512], BF16) for i in range(3)]
                Rb = [fw.sbt(f"a_Rb{i}", [128, 512], BF16) for i in range(2)]
                wb_ = [fw.sbt(f"a_w{i}", [128, 512], BF16) for i in range(2)]
                fw.op(pool, lambda e: e.memset(qTm[0].ap[64:128, :], 0.0), writes=[qTm[0]])
                fw.op(pool, lambda e: e.memset(qTm[1].ap[0:64, :], 0.0), writes=[qTm[1]])
                for hp in range(4):
                    wsl = acquire("attn")
                    Wp = wsl.ap[:, 0:4096].rearrange("p (kb n) -> p kb n", n=512)
                    for tt in range(4):
                        h_ = HT[tt]
                        tok = slice(tt * 512, (tt + 1) * 512)
                        for (bk, c0) in ((B[2], 0), (B[3], 128)):
                            for kb in range(8):
                                fw.op(pe, lambda e: e.matmul(bk.ap, lhsT=Wp[:, kb, c0:c0 + 128], rhs=h_.ap[:, kb, :],
                                                             start=(kb == 0), stop=(kb == 7)), reads=[wsl, h_], writes=[bk])
                        for hh in range(2):
                            ph = hh * 64
                            fw.op(dve, lambda e: e.tensor_scalar(out=qTm[hh].ap[ph:ph + 64, tok], in0=B[2].ap[ph:ph + 64, :], scalar1=0.125,
                                                                 scalar2=None, op0=ALU.mult), reads=[B[2]], writes=[qTm[hh]])
                        fw.op(act, lambda e: e.activation(out=kT.ap[:, tok], in_=B[3].ap, func=AF.Identity), reads=[B[3]], writes=[kT])
                        for kb in range(8):
                            fw.op(pe, lambda e: e.matmul(B[2].ap, lhsT=Wp[:, kb, 384:512], rhs=h_.ap[:, kb, :],
                                                         start=(kb == 0), stop=(kb == 7)), reads=[wsl, h_], writes=[B[2]])
                        for ib in range(4):
                            for kb in range(8):
                                fw.op(pe, lambda e: e.matmul(B[3].ap[:, ib * 128:(ib + 1) * 128], lhsT=h_.ap[:, kb, ib * 128:(ib + 1) * 128],
                                                             rhs=Wp[:, kb, 256:384], start=(kb == 0), stop=(kb == 7)),
                                      reads=[wsl, h_], writes=[B[3]])
                        fw.op(act, lambda e: e.activation(out=zsT.ap[:, tok], in_=B[2].ap, func=AF.Silu), reads=[B[2]], writes=[zsT])
                        fw.op(dve, lambda e: e.tensor_copy(out=vtk.ap[:, tt * 4:(tt + 1) * 4, :], in_=B[3].ap.rearrange("p (i n) -> p i n", n=128)),
                              reads=[B[3]], writes=[vtk])
                    units = []
                    for qt in range(4):
                        for hh in range(2):
                            nkb = 4 * qt + 4
                            grp = qt * 2 + hh
                            for kb in range(nkb - 1, -1, -1):
                                dk = kb - 4 * qt
                                units.append(dict(qt=qt, hh=hh, kb=kb, grp=grp, gi=nkb - 1 - kb, first=(kb == nkb - 1), last=(kb == 0), dk=dk,
                                                  c0=(128 * dk if dk > 0 else 0)))
                    for ui, un in enumerate(units):
                        un["u"] = ui

                    def stage_a(un):
                        u, hh, kb, c0, dk = un["u"], un["hh"], un["kb"], un["c0"], un["dk"]
                        q0 = un["qt"] * 512
                        L, E_, SP_ = B[2 + u % 2], Eb[u % 2], SPb[u % 3]
                        fw.op(pe, lambda e: e.matmul(L.ap[:, c0:512], lhsT=kT.ap[:, kb * 128:(kb + 1) * 128],
                                                     rhs=qTm[hh].ap[:, q0 + c0:q0 + 512], start=True, stop=True),
                              reads=[kT, qTm[hh]], writes=[L])
                        fw.op(act, lambda e: e.activation(out=E_.ap[:, c0:512], in_=L.ap[:, c0:512], func=AF.Exp), reads=[L], writes=[E_])
                        fw.op(act, lambda e: e.activation(out=SP_.ap[:, c0:512], in_=E_.ap[:, c0:512], func=AF.Ln, bias=1.0),
                              reads=[E_], writes=[SP_])
                        if dk >= 0:
                            fw.op(dve, lambda e: e.tensor_tensor(out=SP_.ap[:, c0:512], in0=SP_.ap[:, c0:512], in1=maskA[dk][:, c0:512],
                                                                 op=ALU.mult), reads=[SP_, constb], writes=[SP_])
                        if un["first"] and c0 > 0:
                            fw.op(dve, lambda e: e.memset(SP_.ap[:, 0:c0], 0.0), writes=[SP_])

                    def stage_b(un):
                        u, hh, kb, c0, dk, first = un["u"], un["hh"], un["kb"], un["c0"], un["dk"], un["first"]
                        q0 = un["qt"] * 512
                        ARG, SP_, w_ = B[4 + u % 2], SPb[u % 3], wb_[u % 2]
                        rb_prev, rb_next = Rb[u % 2], Rb[(u + 1) % 2]
                        fw.op(pe, lambda e: e.matmul(ARG.ap[:, c0:512], lhsT=kT.ap[:, kb * 128:(kb + 1) * 128],
                                                     rhs=qTm[hh].ap[:, q0 + c0:q0 + 512], start=True, stop=False),
                              reads=[kT, qTm[hh]], writes=[ARG])
                        fw.op(pe, lambda e: e.matmul(ARG.ap[:, c0:512], lhsT=trineg, rhs=SP_.ap[:, c0:512], start=False, stop=first),
                              reads=[constb, SP_], writes=[ARG])
                        if not first:
                            fw.op(pe, lambda e: e.matmul(ARG.ap[:, c0:512], lhsT=onesneg, rhs=rb_prev.ap[:, c0:512], start=False, stop=True),
                                  reads=[constb, rb_prev], writes=[ARG])
                        if kb > 0:
                            c1 = 128 * (dk - 1) if dk > 1 else 0
                            RB = B[un["grp"] % 2]
                            ra = 0 if first else c0
                            fw.op(pe, lambda e: e.matmul(RB.ap[:, ra:512], lhsT=identb, rhs=SP_.ap[:, ra:512], start=first, stop=True,
                                                         skip_group_check=True), reads=[constb, SP_], writes=[RB])
                            fw.op(dve, lambda e: e.tensor_copy(out=rb_next.ap[:, c1:512], in_=RB.ap[:, c1:512]), reads=[RB], writes=[rb_next])
                        fw.op(act, lambda e: e.activation(out=w_.ap[:, c0:512], in_=ARG.ap[:, c0:512], func=AF.Exp), reads=[ARG], writes=[w_])
                        if dk >= 0:
                            fw.op(dve, lambda e: e.tensor_tensor(out=w_.ap[:, c0:512], in0=w_.ap[:, c0:512], in1=maskA[dk][:, c0:512], op=ALU.mult),
                                  reads=[w_, constb], writes=[w_])
                            if c0 > 0:
                                fw.op(dve, lambda e: e.memset(w_.ap[:, 0:c0], 0.0), writes=[w_])

                    def stage_c(un):
                        u, hh, kb, qt = un["u"], un["hh"], un["kb"], un["qt"]
                        ph = hh * 64
                        OT, w_ = B[6 + hh], wb_[u % 2]
                        fw.op(pe, lambda e: e.matmul(OT.ap, lhsT=vtk.ap[:, kb, :], rhs=w_.ap, start=un["first"], stop=un["last"]),
                              reads=[vtk, w_], writes=[OT])
                        if un["last"]:
                            fw.op(dve, lambda e: e.tensor_tensor(out=YT[2][qt].ap[ph:ph + 64, hp, :], in0=OT.ap[ph:ph + 64, :],
                                                                 in1=zsT.ap[ph:ph + 64, qt * 512:(qt + 1) * 512], op=ALU.mult),
                                  reads=[OT, zsT], writes=[YT[2][qt]])

                    nU = len(units)
                    for i in range(nU + 3):
                        if i < nU:
                            stage_a(units[i])
                        if 0 <= i - 2 < nU:
                            stage_b(units[i - 2])
                        if 0 <= i - 3 < nU:
                            stage_c(units[i - 3])

            if debug and l == n_layers - 1:
                with fw.scope():
                    dsem = fw.new_sem("dsem")
                    fw.dma(pool, dsem, [(dbg_h[:, kb, :], HTf[:, kb, :]) for kb in range(8)], reads=HT)
                    for n in range(3):
                        fw.dma(pool, dsem, [(dbg_y[n][:, wb, :], YTf[n][:, wb, :]) for wb in range(4)], reads=YT[n])

            with fw.scope():
                mg = fw.sbt("m_merged", [128, 8, 1024], BF16)
                sg = [fw.sbt(f"m_sg{i}", [128, 512], BF16) for i in range(3)]
                macc = fw.sbt("m_acc", [128, 512], F32)
                mt = fw.sbt("m_t", [128, 512], F32)
                gbc = fw.sbt("m_gbc", [128, D], F32)
                tmpO = [fw.sbt(f"m_tmpO{i}", [128, 512], F32) for i in range(2)]
                dgt = [fw.sbt(f"m_dg{i}", [128, 128], F32) for i in range(2)]
                for j in range(8):
                    d_ = dgt[j % 2]
                    bk = B[6 + (j // 4)]
                    fw.op(dve, lambda e: e.tensor_scalar(out=d_.ap, in0=identf.ap, scalar1=modT[l][:, 16 + j:17 + j], scalar2=None, op0=ALU.mult),
                          reads=[identf, small], writes=[d_])
                    fw.op(pe, lambda e: e.matmul(bk.ap[:, (j % 4) * 128:(j % 4 + 1) * 128], lhsT=onesf.ap, rhs=d_.ap, start=True, stop=True),
                          reads=[onesf, d_], writes=[bk])
                for jh in range(2):
                    fw.op(act, lambda e: e.activation(out=gbc.ap[:, jh * 512:(jh + 1) * 512], in_=B[6 + jh].ap, func=AF.Identity),
                          reads=[B[6 + jh]], writes=[gbc])
                for half in range(2):
                    for j in range(8):
                        wsl = acquire("merge")
                        Wg = wsl.ap[:, 0:3072].rearrange("p (kb n) -> p kb n", n=384)
                        Wb = wsl.ap[:, 3072:4608].rearrange("p (n wb d) -> p n wb d", n=3, wb=4)
                        for t2 in range(2):
                            tt = half * 2 + t2
                            h_ = HT[tt]
                            for n in range(3):
                                for kb in range(8):
                                    fw.op(pe, lambda e: e.matmul(B[n].ap, lhsT=Wg[:, kb, n * 128:(n + 1) * 128], rhs=h_.ap[:, kb, :],
                                                                 start=(kb == 0), stop=(kb == 7)), reads=[wsl, h_], writes=[B[n]])
                                fw.op(act, lambda e: e.activation(out=sg[n].ap, in_=B[n].ap, func=AF.Sigmoid), reads=[B[n]], writes=[sg[n]])
                                y_ = YT[n][tt]
                                for wb in range(4):
                                    fw.op(pe, lambda e: e.matmul(B[3 + n].ap, lhsT=Wb[:, n, wb, :], rhs=y_.ap[:, wb, :],
                                                                 start=(wb == 0), stop=(wb == 3)), reads=[wsl, y_], writes=[B[3 + n]])
                            fw.op(dve, lambda e: e.tensor_tensor(out=macc.ap, in0=B[3].ap, in1=sg[0].ap, op=ALU.mult), reads=[B[3], sg[0]], writes=[macc])
                            fw.op(dve, lambda e: e.tensor_tensor(out=mt.ap, in0=B[4].ap, in1=sg[1].ap, op=ALU.mult), reads=[B[4], sg[1]], writes=[mt])
                            fw.op(dve, lambda e: e.tensor_tensor(out=macc.ap, in0=macc.ap, in1=mt.ap, op=ALU.add), reads=[macc, mt], writes=[macc])
                            fw.op(dve, lambda e: e.tensor_tensor(out=mt.ap, in0=B[5].ap, in1=sg[2].ap, op=ALU.mult), reads=[B[5], sg[2]], writes=[mt])
                            fw.op(dve, lambda e: e.tensor_tensor(out=mg.ap[:, j, t2 * 512:(t2 + 1) * 512], in0=macc.ap, in1=mt.ap, op=ALU.add),
                                  reads=[macc, mt], writes=[mg])
                    u = 0
                    for cc in range(2):
                        wsl = acquire("outp")
                        Wo = wsl.ap[:, 0:4096].rearrange("p (kb n) -> p kb n", n=512)
                        for i8 in range(8):
                            i = half * 8 + i8
                            bk = B[6 + u % 2]
                            to = tmpO[u % 2]
                            u += 1
                            for kb in range(8):
                                fw.op(pe, lambda e: e.matmul(bk.ap, lhsT=mg.ap[:, kb, i8 * 128:(i8 + 1) * 128], rhs=Wo[:, kb, :],
                                                             start=(kb == 0), stop=(kb == 7)), reads=[mg, wsl], writes=[bk])
                            fw.op(dve, lambda e: e.tensor_tensor(out=to.ap, in0=bk.ap, in1=gbc.ap[:, cc * 512:(cc + 1) * 512], op=ALU.mult),
                                  reads=[bk, gbc], writes=[to])
                            fw.op(pool, lambda e: e.tensor_tensor(out=XR[i].ap[:, cc * 512:(cc + 1) * 512], in0=XR[i].ap[:, cc * 512:(cc + 1) * 512],
                                                                  in1=to.ap, op=ALU.add), reads=[XR[i], to], writes=[XR[i]])

        with fw.scope():
            osem = [fw.new_sem(f"osem{i}") for i in range(2)]
            if final_norm:
                fnw = fw.sbt("fnw", [128, D], F32)
                junk = fw.sbt("junk2", [128, D], BF16)
                ot = [fw.sbt(f"ot{i}", [128, D], F32) for i in range(2)]
                fw.dma(sp, csem[3], [(fnw.ap, fnw_d)], writes=[fnw])
                for i in range(16):
                    fw.op(act, lambda e: e.activation(out=junk.ap, in_=XR[i].ap, func=AF.Square, accum_out=ss[:, i:i + 1]),
                          reads=[XR[i]], writes=[junk, small])
                fw.op(dve, lambda e: e.tensor_scalar(out=ss, in0=ss, scalar1=1.0 / D, scalar2=1e-6, op0=ALU.mult, op1=ALU.add),
                      reads=[small], writes=[small])
                fw.op(pool, lambda e: e.tensor_tensor(out=rstd, in0=ss, in1=mhalf, op=ALU.pow), reads=[small], writes=[small])
                for i in range(16):
                    o_ = ot[i % 2]
                    fw.op(dve, lambda e: e.scalar_tensor_tensor(out=o_.ap, in0=XR[i].ap, scalar=rstd[:, i:i + 1], in1=fnw.ap,
                                                                op0=ALU.mult, op1=ALU.mult), reads=[XR[i], small, fnw], writes=[o_])
                    fw.dma(sp, osem[i % 2], [(out_d[i * 128:(i + 1) * 128, :], o_.ap)], reads=[o_])
                fw.barrier(ot)
            else:
                for i in range(16):
                    fw.dma(sp, osem[i % 2], [(out_d[i * 128:(i + 1) * 128, :], XR[i].ap)], reads=[XR[i]])
                fw.barrier(XR)
        stats = {e.name: (e.n_ins, e.n_wait) for e in fw.engines}
        print("instr/waits", stats, "sems", fw.n_sems)
    return nc


def _pp(v, nb):
    return np.ascontiguousarray(np.asarray(v, np.float32).reshape(nb, 128).T)


def _host_consts():
    identf = np.eye(128, dtype=np.float32)
    cb = np.zeros((128, NCB), np.float32)
    cb[:, 0:128] = identf
    j = np.arange(128)[:, None]
    s = np.arange(128)[None, :]
    cb[:, 128:256] = -(j >= s).astype(np.float32)
    cb[:, 256:384] = -1.0
    cb[:, 384:512] = 1.0
    t64 = np.arange(64)[None, :]
    cb[:, 512:576] = ((np.arange(128)[:, None] % 64) <= t64).astype(np.float32)
    t512 = np.arange(512)[None, :]
    for dk in range(4):
        cb[:, 576 + dk * 512:576 + (dk + 1) * 512] = ((np.arange(128)[:, None] + 128 * dk) < t512).astype(np.float32)
    return identf, cb


def _vecs(b, c, ada_b, norm_w, hgrn_lb, hgrn_norm_w, conv_w, conv_b, conv_ln_w, conv_ln_b):
    v = np.zeros((128, NV), np.float32)
    for l in range(2):
        o = l * LV
        v[:, o:o + 24] = _pp(ada_b[l], 24)
        v[:, o + 24:o + 32] = _pp(norm_w[l], 8)
        v[:, o + 32:o + 36] = _pp(hgrn_lb[l], 4)
        v[:, o + 36] = np.asarray(hgrn_norm_w[l], np.float32)
        v[:, o + 37:o + 161] = np.asarray(conv_w[l], np.float32).T.reshape(4, 128, 31).transpose(1, 0, 2).reshape(128, 124)
        v[:, o + 161:o + 165] = _pp(conv_b[l], 4)
        v[:, o + 165:o + 169] = _pp(conv_ln_w[l], 4)
        v[:, o + 169:o + 173] = _pp(conv_ln_b[l], 4)
    cT = _pp(c[b], 8)
    v[:, 2 * LV:2 * LV + 16] = np.repeat(cT[:, :, None], 2, axis=2).reshape(128, 16)
    return v


def _in_maps(x, c, ada_w, ada_b, norm_w, w_in, hgrn_lb, hgrn_norm_w, conv_w, conv_b, conv_ln_w, conv_ln_b,
             w_branch, w_out, final_norm_w, cores):
    identf, cb = _host_consts()
    f = lambda a: np.ascontiguousarray(np.asarray(a, np.float32))
    ada_w, w_in, w_branch, w_out = f(ada_w), f(w_in), f(w_branch), f(w_out)
    fnw = np.ascontiguousarray(np.broadcast_to(np.asarray(final_norm_w, np.float32)[None, :], (128, D)))
    x = np.asarray(x, np.float32)
    maps = []
    for b in cores:
        maps.append({
            "x": np.ascontiguousarray(x[b]),
            "vecs": _vecs(b, np.asarray(c, np.float32), ada_b, norm_w, hgrn_lb, hgrn_norm_w, conv_w, conv_b, conv_ln_w, conv_ln_b),
            "identf": identf, "constb": cb, "fnw": fnw,
            "ada_w": ada_w, "w_in": w_in, "w_branch": w_branch, "w_out": w_out,
        })
    return maps


def kernel(x, c, ada_w, ada_b, norm_w, w_in, hgrn_lb, hgrn_norm_w, conv_w, conv_b, conv_ln_w, conv_ln_b,
           w_branch, w_out, final_norm_w):
    nc = build(2, True, False)
    maps = _in_maps(x, c, ada_w, ada_b, norm_w, w_in, hgrn_lb, hgrn_norm_w, conv_w, conv_b, conv_ln_w, conv_ln_b,
                    w_branch, w_out, final_norm_w, list(range(8)))
    res = run_bass_kernel_spmd(nc, maps, core_ids=list(range(8)))
    return np.stack([np.asarray(r["out"], np.float32) for r in res.results], axis=0)
```

```python
import numpy as np
from contextlib import ExitStack, contextmanager
import concourse.bass as bass
import concourse.mybir as mybir
from concourse.bass_utils import run_bass_kernel_spmd

F32 = mybir.dt.float32
BF16 = mybir.dt.bfloat16
AF = mybir.ActivationFunctionType
ALU = mybir.AluOpType

S = 2048
D = 1024
W = 512
NCOL = 8704
SEM_LIMIT = 12000
LV = 173
NV = 2 * LV + 16
NCB = 4 * 128 + 64 + 4 * 512


class SemObj:
    __slots__ = ("sem", "count", "name")

    def __init__(self, sem, name):
        self.sem = sem
        self.count = 0
        self.name = name


class Engine:
    def __init__(self, fw, name, h, is_pe=False, compute=True):
        self.fw = fw
        self.name = name
        self.h = h
        self.is_pe = is_pe
        self.own = []
        self.cur = None
        self.seen = {}
        self.n_wait = 0
        self.n_ins = 0
        if compute:
            self._new_sem()

    def _new_sem(self):
        s = self.fw.new_sem(f"{self.name}_s{len(self.own)}")
        self.own.append(s)
        self.cur = s


class Tile:
    __slots__ = ("ap", "name", "last_write", "reads")

    def __init__(self, ap, name=""):
        self.ap = ap
        self.name = name
        self.last_write = None
        self.reads = {}


class FW:
    def __init__(self, nc, stack):
        self.nc = nc
        self.stack = stack
        self.stacks = [stack]
        self.scope_tiles = [[]]
        self.n_sems = 0
        self.pe = Engine(self, "pe", nc.tensor, is_pe=True)
        self.act = Engine(self, "act", nc.scalar)
        self.dve = Engine(self, "dve", nc.vector)
        self.pool = Engine(self, "pool", nc.gpsimd)
        self.sp = Engine(self, "sp", nc.sync, compute=False)
        self.engines = [self.pe, self.act, self.dve, self.pool, self.sp]

    def new_sem(self, name):
        self.n_sems += 1
        return SemObj(self.stack.enter_context(self.nc.semaphore(name)), name)

    def sbt(self, name, shape, dtype):
        self.n_sb = getattr(self, "n_sb", 0) + 1
        t = self.stacks[-1].enter_context(self.nc.sbuf_tensor(f"sb{self.n_sb}_{name}", list(shape), dtype))
        tl = Tile(t[:], name)
        self.scope_tiles[-1].append(tl)
        return tl

    def sub(self, ap, name=""):
        tl = Tile(ap, name)
        self.scope_tiles[-1].append(tl)
        return tl

    @contextmanager
    def scope(self):
        st = ExitStack()
        self.stacks.append(st)
        self.scope_tiles.append([])
        try:
            yield
        finally:
            tiles = self.scope_tiles.pop()
            self.barrier(tiles)
            self.stacks.pop()
            st.close()

    def barrier(self, tiles):
        for eng in self.engines:
            self._deps(eng, tiles, tiles)

    def _deps(self, eng, reads, writes):
        deps = {}

        def add(s, v, kind):
            if s in eng.own:
                if eng.is_pe or kind != "raw":
                    return
            if deps.get(s, 0) < v:
                deps[s] = v

        for t in reads:
            if t.last_write is not None:
                add(t.last_write[0], t.last_write[1], "raw")
        for t in writes:
            if t.last_write is not None:
                add(t.last_write[0], t.last_write[1], "waw")
            for s, v in t.reads.items():
                add(s, v, "war")
        for s, v in deps.items():
            if eng.seen.get(s, 0) >= v:
                continue
            eng.h.wait_ge(s.sem, v)
            eng.seen[s] = v
            eng.n_wait += 1

    def _mark(self, mark, reads, writes):
        for t in reads:
            if t.reads.get(mark[0], 0) < mark[1]:
                t.reads[mark[0]] = mark[1]
        for t in writes:
            t.last_write = mark
            t.reads = {}

    def op(self, eng, fn, reads=(), writes=()):
        self._deps(eng, reads, writes)
        if eng.cur.count >= SEM_LIMIT:
            eng._new_sem()
        ins = fn(eng.h)
        ins.then_inc(eng.cur.sem, 1)
        eng.cur.count += 1
        eng.n_ins += 1
        self._mark((eng.cur, eng.cur.count), reads, writes)
        return ins

    def dma(self, q, semobj, pairs, reads=(), writes=()):
        self._deps(q, reads, writes)
        for out, in_ in pairs:
            q.h.dma_start(out=out, in_=in_).then_inc(semobj.sem, 16)
            semobj.count += 16
            q.n_ins += 1
        self._mark((semobj, semobj.count), reads, writes)


def build(n_layers=2, final_norm=True, debug=False):
    nc = bass.Bass("TRN2", target_bir_lowering=False)
    x_d = nc.dram_tensor("x", [S, D], F32, kind="ExternalInput").ap()
    vecs_d = nc.dram_tensor("vecs", [128, NV], F32, kind="ExternalInput").ap()
    identf_d = nc.dram_tensor("identf", [128, 128], F32, kind="ExternalInput").ap()
    constb_d = nc.dram_tensor("constb", [128, NCB], F32, kind="ExternalInput").ap()
    fnw_d = nc.dram_tensor("fnw", [128, D], F32, kind="ExternalInput").ap()
    ada_w_d = nc.dram_tensor("ada_w", [2, D, 3 * D], F32, kind="ExternalInput").ap()
    w_in_d = nc.dram_tensor("w_in", [2, D, NCOL], F32, kind="ExternalInput").ap()
    w_br_d = nc.dram_tensor("w_branch", [2, 3, W, D], F32, kind="ExternalInput").ap()
    w_out_d = nc.dram_tensor("w_out", [2, D, D], F32, kind="ExternalInput").ap()
    out_d = nc.dram_tensor("out", [S, D], F32, kind="ExternalOutput").ap()
    if debug:
        dbg_h = nc.dram_tensor("dbg_h", [128, 8, S], F32, kind="ExternalOutput").ap()
        dbg_y = nc.dram_tensor("dbg_y", [3, 128, 4, S], F32, kind="ExternalOutput").ap()

    with ExitStack() as st:
        fw = FW(nc, st)
        pe, act, dve, pool, sp = fw.pe, fw.act, fw.dve, fw.pool, fw.sp

        XRf = st.enter_context(nc.sbuf_tensor("x_res", [128, 16, D], F32))
        XR = [fw.sub(XRf[:, i, :], f"xr{i}") for i in range(16)]
        HTf = st.enter_context(nc.sbuf_tensor("hT", [128, 8, S], BF16))
        HT = [fw.sub(HTf[:, :, tt * 512:(tt + 1) * 512], f"ht{tt}") for tt in range(4)]
        YTf = [st.enter_context(nc.sbuf_tensor(f"yT{n}", [128, 4, S], BF16)) for n in range(3)]
        YT = [[fw.sub(YTf[n][:, :, tt * 512:(tt + 1) * 512], f"yt{n}_{tt}") for tt in range(4)] for n in range(3)]
        WS = [fw.sbt(f"wslot{i}", [128, 4608], BF16) for i in range(2)]
        WSEM = [fw.new_sem(f"wsem{i}") for i in range(2)]
        vecs = fw.sbt("vecs", [128, NV], F32)
        identf = fw.sbt("identf", [128, 128], F32)
        onesf = fw.sbt("onesf", [128, 128], F32)
        constb = fw.sbt("constb", [128, NCB], BF16)
        identb = constb.ap[:, 0:128]
        trineg = constb.ap[:, 128:256]
        onesneg = constb.ap[:, 256:384]
        onesb = constb.ap[:, 384:512]
        maskH = constb.ap[:, 512:576]
        maskA = [constb.ap[:, 576 + dk * 512:576 + (dk + 1) * 512] for dk in range(4)]
        small = fw.sbt("small", [128, 256], F32)
        ss = small.ap[:, 0:16]
        rstd = small.ap[:, 16:32]
        mhalf = small.ap[:, 32:48]
        cact = small.ap[:, 48:64]
        modT = [small.ap[:, 64 + l * 24:64 + (l + 1) * 24] for l in range(2)]
        Avec = [small.ap[:, 112 + l * 8:112 + (l + 1) * 8] for l in range(2)]
        c1v = [small.ap[:, 128 + l * 4:128 + (l + 1) * 4] for l in range(2)]
        c0v = [small.ap[:, 136 + l * 4:136 + (l + 1) * 4] for l in range(2)]
        sm_tmp = small.ap[:, 144:176]
        cwhT = fw.sbt("cwh", [128, 2 * 124], F32)
        psall = st.enter_context(nc.psum_tensor("psall", [128, 8 * 512], F32))
        B = [fw.sub(psall[:, i * 512:(i + 1) * 512], f"bank{i}") for i in range(8)]
        XSEM = [fw.new_sem(f"xsem{i}") for i in range(16)]
        csem = [fw.new_sem(f"csem{i}") for i in range(4)]

        def vcol(l, off, n=1):
            return vecs.ap[:, l * LV + off:l * LV + off + n]

        fw.dma(sp, csem[0], [(vecs.ap, vecs_d)], writes=[vecs])
        fw.dma(sp, csem[1], [(identf.ap, identf_d)], writes=[identf])
        fw.dma(pool, csem[2], [(constb.ap[:, 0:1312], constb_d[:, 0:1312]), (constb.ap[:, 1312:NCB], constb_d[:, 1312:NCB])], writes=[constb])
        x_loads = [(lambda i=i: fw.dma(sp, XSEM[i], [(XR[i].ap, x_d[i * 128:(i + 1) * 128, :])], writes=[XR[i]])) for i in range(16)]
        fw.op(pool, lambda e: e.memset(onesf.ap, 1.0), writes=[onesf])
        fw.op(pool, lambda e: e.memset(mhalf, -0.5), writes=[small])

        jobs = []
        for l in range(n_layers):
            for hd in range(4):
                jobs.append(("hgrn", l, hd))
            for cb in range(4):
                jobs.append(("conv", l, cb))
            jobs.append(("convz", l, 0))
            for hp in range(4):
                jobs.append(("attn", l, hp))
            for half in range(2):
                for j in range(8):
                    jobs.append(("merge", l, j))
                for cc in range(2):
                    jobs.append(("outp", l, cc))
        issued = [0]

        def wsrc(l, c0, n):
            return w_in_d[l, :, c0:c0 + n].rearrange("(kb p) n -> p kb n", p=128)

        def issue(k):
            kind, l, idx = jobs[k]
            slot = WS[k % 2]
            v512 = slot.ap[:, 0:4096].rearrange("p (kb n) -> p kb n", n=512)
            pairs = []
            if kind == "hgrn":
                for b in range(4):
                    pairs.append((v512[:, :, b * 128:(b + 1) * 128], wsrc(l, b * 512 + idx * 128, 128)))
            elif kind == "conv":
                pairs.append((v512[:, :, 0:128], wsrc(l, 2048 + idx * 128, 128)))
                pairs.append((v512[:, :, 128:256], wsrc(l, 2560 + idx * 128, 128)))
            elif kind == "convz":
                pairs.append((v512, wsrc(l, 3072, 512)))
            elif kind == "attn":
                for b in range(4):
                    pairs.append((v512[:, :, b * 128:(b + 1) * 128], wsrc(l, 3584 + b * 512 + idx * 128, 128)))
            elif kind == "merge":
                vg = slot.ap[:, 0:3072].rearrange("p (kb n) -> p kb n", n=384)
                vb = slot.ap[:, 3072:4608].rearrange("p (n wb d) -> p n wb d", n=3, wb=4)
                for n in range(3):
                    pairs.append((vg[:, :, n * 128:(n + 1) * 128], wsrc(l, 5632 + n * 1024 + idx * 128, 128)))
                    pairs.append((vb[:, n, :, :],
                                  w_br_d[l, n, :, idx * 128:(idx + 1) * 128].rearrange("(wb p) d -> p wb d", p=128)))
            elif kind == "outp":
                pairs.append((v512, w_out_d[l, :, idx * 512:(idx + 1) * 512].rearrange("(kb p) n -> p kb n", p=128)))
            fw.dma(pool, WSEM[k % 2], pairs, writes=[slot])

        def acquire(expect):
            k = acquire.k
            assert jobs[k][0] == expect, (jobs[k], expect)
            while issued[0] <= min(k + 1, len(jobs) - 1):
                issue(issued[0])
                issued[0] += 1
            acquire.k += 1
            return WS[k % 2]

        acquire.k = 0

        cT2 = vecs.ap[:, 2 * LV:2 * LV + 16]
        fw.op(act, lambda e: e.activation(out=cact, in_=cT2, func=AF.Silu), reads=[vecs], writes=[small])
        def mod_steps(l, AW, awsem, bank):
            cact3 = cact.rearrange("p (k t) -> p k t", t=2)
            steps = []
            for jg in range(6):
                aw = AW[jg % 2]

                def ld(jg=jg, aw=aw):
                    fw.dma(sp, awsem[jg % 2],
                           [(aw.ap, ada_w_d[l, :, jg * 512:(jg + 1) * 512].rearrange("(kb p) n -> p kb n", p=128))],
                           writes=[aw])
                steps.append(ld)
                for jj in range(4):
                    def mm(jg=jg, jj=jj, aw=aw):
                        col = 2 * (jg * 4 + jj)
                        for kb in range(8):
                            fw.op(pe, lambda e: e.matmul(bank.ap[:, col:col + 2], lhsT=aw.ap[:, kb, jj * 128:(jj + 1) * 128],
                                                         rhs=cact3[:, kb, :], start=(kb == 0), stop=(kb == 7)),
                                  reads=[aw, small], writes=[bank])
                    steps.append(mm)
            return steps

        def mod_finish(l, bank):
            src = bank.ap[:, 0:48].rearrange("p (j t) -> p j t", t=2)[:, :, 0]
            fw.op(dve, lambda e: e.tensor_tensor(out=modT[l], in0=src, in1=vcol(l, 0, 24), op=ALU.add),
                  reads=[bank, vecs], writes=[small])
            fw.op(dve, lambda e: e.scalar_tensor_tensor(out=Avec[l], in0=modT[l][:, 8:16], scalar=1.0, in1=vcol(l, 24, 8),
                                                        op0=ALU.add, op1=ALU.mult), reads=[small, vecs], writes=[small])

        with fw.scope():
            AW = [fw.sbt(f"aw{i}", [128, 8, 512], F32) for i in range(2)]
            awsem = [fw.new_sem(f"awsem{i}") for i in range(2)]
            n_ld = 0
            for f_ in mod_steps(0, AW, awsem, B[0]):
                f_()
                if getattr(f_, "__name__", "") == "ld":
                    n_ld += 1
                    if n_ld >= 2:
                        for _ in range(4):
                            if x_loads:
                                x_loads.pop(0)()
            while x_loads:
                x_loads.pop(0)()
            mod_finish(0, B[0])
            a0 = vcol(0, 32, 4)
            a1 = vcol(1, 32, 4)
            t_mx, t_e0, t_e1, t_s, t_p0, t_p1, t_lo = [sm_tmp[:, 4 * i:4 * i + 4] for i in range(7)]
            fw.op(dve, lambda e: e.tensor_tensor(out=t_mx, in0=a0, in1=a1, op=ALU.max), reads=[vecs], writes=[small])
            fw.op(dve, lambda e: e.tensor_tensor(out=t_e0, in0=a0, in1=t_mx, op=ALU.subtract), reads=[vecs, small], writes=[small])
            fw.op(dve, lambda e: e.tensor_tensor(out=t_e1, in0=a1, in1=t_mx, op=ALU.subtract), reads=[vecs, small], writes=[small])
            fw.op(act, lambda e: e.activation(out=t_e0, in_=t_e0, func=AF.Exp), reads=[small], writes=[small])
            fw.op(act, lambda e: e.activation(out=t_e1, in_=t_e1, func=AF.Exp), reads=[small], writes=[small])
            fw.op(dve, lambda e: e.tensor_tensor(out=t_s, in0=t_e0, in1=t_e1, op=ALU.add), reads=[small], writes=[small])
            fw.op(dve, lambda e: e.reciprocal(out=t_s, in_=t_s), reads=[small], writes=[small])
            fw.op(dve, lambda e: e.tensor_tensor(out=t_p0, in0=t_e0, in1=t_s, op=ALU.mult), reads=[small], writes=[small])
            fw.op(dve, lambda e: e.tensor_tensor(out=t_p1, in0=t_e1, in1=t_s, op=ALU.mult), reads=[small], writes=[small])
            for l in range(2):
                if l == 0:
                    fw.op(dve, lambda e: e.tensor_tensor(out=t_lo, in0=t_p0, in1=t_p0, op=ALU.subtract), reads=[small], writes=[small])
                else:
                    fw.op(dve, lambda e: e.tensor_tensor(out=t_lo, in0=t_p0, in1=t_p1, op=ALU.add), reads=[small], writes=[small])
                    fw.op(dve, lambda e: e.tensor_tensor(out=t_lo, in0=t_lo, in1=t_p0, op=ALU.subtract), reads=[small], writes=[small])
                fw.op(dve, lambda e: e.tensor_scalar(out=c1v[l], in0=t_lo, scalar1=-0.5, scalar2=0.5, op0=ALU.mult, op1=ALU.add),
                      reads=[small], writes=[small])
                fw.op(dve, lambda e: e.tensor_scalar(out=c0v[l], in0=t_lo, scalar1=0.5, scalar2=0.5, op0=ALU.mult, op1=ALU.add),
                      reads=[small], writes=[small])
            for l in range(2):
                fw.op(dve, lambda e: e.tensor_scalar(out=cwhT.ap[:, l * 124:(l + 1) * 124], in0=vcol(l, 37, 124), scalar1=0.5,
                                                     scalar2=None, op0=ALU.mult), reads=[vecs], writes=[cwhT])

        for l in range(n_layers):
            with fw.scope():
                junk = fw.sbt("junk", [128, D], BF16)
                dg = [fw.sbt(f"dg{i}", [128, 128], F32) for i in range(2)]
                msteps = []
                if l + 1 < n_layers:
                    AW2 = [fw.sbt(f"aw2{i}", [128, 8, 512], F32) for i in range(2)]
                    awsem2 = [fw.new_sem(f"awsem2_{l}_{i}") for i in range(2)]
                    msteps = mod_steps(l + 1, AW2, awsem2, B[4])
                mi = 0
                for i in range(16):
                    fw.op(act, lambda e: e.activation(out=junk.ap, in_=XR[i].ap, func=AF.Square, accum_out=ss[:, i:i + 1]),
                          reads=[XR[i]], writes=[junk, small])
                fw.op(dve, lambda e: e.tensor_scalar(out=ss, in0=ss, scalar1=1.0 / D, scalar2=1e-6, op0=ALU.mult, op1=ALU.add),
                      reads=[small], writes=[small])
                fw.op(pool, lambda e: e.tensor_tensor(out=rstd, in0=ss, in1=mhalf, op=ALU.pow), reads=[small], writes=[small])
                for i in range(16):
                    d_ = dg[i % 2]
                    fw.op(dve, lambda e: e.tensor_scalar(out=d_.ap, in0=identf.ap, scalar1=rstd[:, i:i + 1], scalar2=None, op0=ALU.mult),
                          reads=[identf, small], writes=[d_])
                    for jh in range(2):
                        bk = B[(2 * i + jh) % 4]
                        for jj in range(4):
                            j = jh * 4 + jj
                            fw.op(pe, lambda e: e.matmul(bk.ap[:, jj * 128:(jj + 1) * 128], lhsT=XR[i].ap[:, j * 128:(j + 1) * 128],
                                                         rhs=d_.ap, start=True, stop=True), reads=[XR[i], d_], writes=[bk])
                        for jj in range(4):
                            j = jh * 4 + jj
                            o_ap = HTf[:, j, i * 128:(i + 1) * 128]
                            i_ap = bk.ap[:, jj * 128:(jj + 1) * 128]
                            if jj % 2 == 0:
                                fw.op(act, lambda e: e.activation(out=o_ap, in_=i_ap, func=AF.Identity, scale=Avec[l][:, j:j + 1],
                                                                  bias=modT[l][:, j:j + 1]), reads=[bk, small], writes=[HT[i // 4]])
                            else:
                                fw.op(dve, lambda e: e.tensor_scalar(out=o_ap, in0=i_ap, scalar1=Avec[l][:, j:j + 1],
                                                                     scalar2=modT[l][:, j:j + 1], op0=ALU.mult, op1=ALU.add),
                                      reads=[bk, small], writes=[HT[i // 4]])
                    for _ in range(2):
                        if mi < len(msteps):
                            msteps[mi]()
                            mi += 1
                while mi < len(msteps):
                    msteps[mi]()
                    mi += 1
                if l + 1 < n_layers:
                    mod_finish(l + 1, B[4])

            with fw.scope():
                qf = fw.sbt("h_qf", [128, 512], F32)
                tf = fw.sbt("h_tf", [128, 512], F32)
                kf = fw.sbt("h_kf", [128, 512], F32)
                bb = fw.sbt("h_bb", [128, 512], F32)
                e1 = fw.sbt("h_e1", [128, 512], F32)
                e2 = fw.sbt("h_e2", [128, 512], F32)
                r1 = fw.sbt("h_r1", [128, 512], F32)
                r2 = fw.sbt("h_r2", [128, 512], F32)
                o2 = fw.sbt("h_o2", [128, 512], BF16)
                ones64 = fw.sbt("h_ones64", [128, 64], F32)
                PB = []
                for i in range(2):
                    PB.append(dict(
                        zs=fw.sbt(f"h_zs{i}", [128, 512], BF16), qtl=fw.sbt(f"h_qtl{i}", [128, 512], BF16),
                        ktl=fw.sbt(f"h_ktl{i}", [128, 512], BF16), ktok=fw.sbt(f"h_ktok{i}", [128, 4, 128], BF16),
                        vtok=fw.sbt(f"h_vtok{i}", [128, 4, 128], BF16), hs=fw.sbt(f"h_small{i}", [128, 56], F32)))
                attnT = [fw.sbt(f"h_attnT{i}", [128, 64], BF16) for i in range(2)]
                aTc = [fw.sbt(f"h_aTc{i}", [128, 64], F32) for i in range(2)]
                Sst = [fw.sbt(f"h_S{i}", [128, 128], F32) for i in range(2)]
                Sb = [fw.sbt(f"h_Sb{i}", [128, 128], BF16) for i in range(2)]
                tmpU = [fw.sbt(f"h_tmpU{i}", [128, 128], F32) for i in range(2)]
                Up = [fw.sub(B[6].ap[:, c * 128:(c + 1) * 128], f"up{c}") for c in range(2)]
                ATp = [fw.sub(B[6].ap[:, 256 + c * 64:256 + (c + 1) * 64], f"atp{c}") for c in range(4)]
                fw.barrier([B[6]])
                fw.op(pool, lambda e: e.memset(ones64.ap, 1.0), writes=[ones64])
                nw = vcol(l, 36, 1)
                wcur = {}

                def prep_steps(g):
                    hd, tt = divmod(g, 4)
                    P = PB[g % 2]
                    zs, qtl, ktl, ktok, vtok, hs = P["zs"], P["qtl"], P["ktl"], P["ktok"], P["vtok"], P["hs"]
                    bm, bl, dlm = hs.ap[:, 0:8], hs.ap[:, 8:16], hs.ap[:, 16:24]
                    h_ = HT[tt]
                    st_ = []

                    def s_acq():
                        if tt == 0:
                            wcur[hd] = acquire("hgrn")
                    st_.append(s_acq)

                    def mm_block(bk, c0):
                        def f():
                            wsl = wcur[hd]
                            Wh = wsl.ap[:, 0:4096].rearrange("p (kb n) -> p kb n", n=512)
                            for kb in range(8):
                                fw.op(pe, lambda e: e.matmul(bk.ap, lhsT=Wh[:, kb, c0:c0 + 128], rhs=h_.ap[:, kb, :],
                                                             start=(kb == 0), stop=(kb == 7)), reads=[wsl, h_], writes=[bk])
                        return f
                    def mm_v(ib):
                        def f():
                            wsl = wcur[hd]
                            Wh = wsl.ap[:, 0:4096].rearrange("p (kb n) -> p kb n", n=512)
                            for kb in range(8):
                                fw.op(pe, lambda e: e.matmul(B[3].ap[:, ib * 128:(ib + 1) * 128], lhsT=h_.ap[:, kb, ib * 128:(ib + 1) * 128],
                                                             rhs=Wh[:, kb, 256:384], start=(kb == 0), stop=(kb == 7)),
                                      reads=[wsl, h_], writes=[B[3]])
                        return f

                    def scans(c0_):
                        def f():
                            for c in range(c0_, c0_ + 4):
                                fw.op(dve, lambda e: e.tensor_tensor_scan(out=bb.ap[:, c * 64:(c + 1) * 64], data0=ones64.ap,
                                                                          data1=e1.ap[:, c * 64:(c + 1) * 64], initial=0.0,
                                                                          op0=ALU.mult, op1=ALU.add), reads=[ones64, e1], writes=[bb])
                        return f
                    bb3 = bb.ap.rearrange("p (c l) -> p c l", l=64)

                    def smalls():
                        fw.op(dve, lambda e: e.tensor_copy(out=bm, in_=bb3[:, :, 31]), reads=[bb], writes=[hs])
                        fw.op(dve, lambda e: e.tensor_copy(out=bl, in_=bb3[:, :, 63]), reads=[bb], writes=[hs])
                        fw.op(dve, lambda e: e.tensor_tensor(out=dlm, in0=bl, in1=bm, op=ALU.subtract), reads=[hs], writes=[hs])

                    def kappa():
                        hprev = PB[(g + 1) % 2]["hs"]
                        fw.op(dve, lambda e: e.tensor_tensor(out=hs.ap[:, 49:56], in0=hs.ap[:, 25:32], in1=hs.ap[:, 40:47], op=ALU.mult),
                              reads=[hs], writes=[hs])
                        if tt == 0:
                            fw.op(dve, lambda e: e.tensor_copy(out=hs.ap[:, 48:49], in_=hs.ap[:, 24:25]), reads=[hs], writes=[hs])
                        else:
                            fw.op(dve, lambda e: e.tensor_tensor(out=hs.ap[:, 48:49], in0=hs.ap[:, 24:25], in1=hprev.ap[:, 47:48], op=ALU.mult),
                                  reads=[hs, hprev], writes=[hs])

                    st_.append(mm_block(B[1], 128))
                    st_.append(lambda: fw.op(act, lambda e: e.activation(out=tf.ap, in_=B[1].ap, func=AF.Tanh, scale=0.5), reads=[B[1]], writes=[tf]))
                    st_.append(lambda: fw.op(dve, lambda e: e.tensor_scalar(out=tf.ap, in0=tf.ap, scalar1=c1v[l][:, hd:hd + 1],
                                                                            scalar2=c0v[l][:, hd:hd + 1], op0=ALU.mult, op1=ALU.add),
                                             reads=[tf, small], writes=[tf]))
                    st_.append(mm_block(B[0], 0))
                    st_.append(lambda: fw.op(act, lambda e: e.activation(out=qf.ap, in_=B[0].ap, func=AF.Silu), reads=[B[0]], writes=[qf]))
                    st_.append(lambda: fw.op(act, lambda e: e.activation(out=kf.ap, in_=tf.ap, func=AF.Identity, scale=-1.0, bias=1.0),
                                             reads=[tf], writes=[kf]))
                    st_.append(lambda: fw.op(dve, lambda e: e.tensor_scalar_max(out=bb.ap, in0=tf.ap, scalar1=1e-30), reads=[tf], writes=[bb]))
                    st_.append(mm_block(B[2], 384))
                    st_.append(lambda: fw.op(act, lambda e: e.activation(out=zs.ap, in_=B[2].ap, func=AF.Silu), reads=[B[2]], writes=[zs]))
                    st_.append(lambda: fw.op(act, lambda e: e.activation(out=e1.ap, in_=bb.ap, func=AF.Ln), reads=[bb], writes=[e1]))
                    st_.append(mm_v(0))
                    st_.append(scans(0))
                    st_.append(mm_v(1))
                    st_.append(scans(4))
                    st_.append(mm_v(2))
                    st_.append(smalls)
                    st_.append(mm_v(3))
                    st_.append(lambda: fw.op(dve, lambda e: e.tensor_tensor(out=tf.ap.rearrange("p (c l) -> p c l", l=64), in0=bb3,
                                                                            in1=bm.unsqueeze(2).to_broadcast([128, 8, 64]), op=ALU.subtract),
                                             reads=[bb, hs], writes=[tf]))
                    st_.append(lambda: fw.op(dve, lambda e: e.tensor_copy(out=vtok.ap, in_=B[3].ap.rearrange("p (i n) -> p i n", n=128)),
                                             reads=[B[3]], writes=[vtok]))
                    st_.append(lambda: fw.op(dve, lambda e: e.tensor_scalar(out=tf.ap, in0=tf.ap, scalar1=80.0, scalar2=-80.0, op0=ALU.min, op1=ALU.max),
                                             reads=[tf], writes=[tf]))
                    st_.append(lambda: fw.op(act, lambda e: e.activation(out=e1.ap, in_=tf.ap, func=AF.Exp), reads=[tf], writes=[e1]))
                    st_.append(lambda: fw.op(act, lambda e: e.activation(out=e2.ap, in_=tf.ap, func=AF.Exp, scale=-1.0), reads=[tf], writes=[e2]))
                    st_.append(lambda: fw.op(act, lambda e: e.activation(out=hs.ap[:, 24:48], in_=hs.ap[:, 0:24], func=AF.Exp), reads=[hs], writes=[hs]))
                    st_.append(kappa)
                    st_.append(lambda: fw.op(dve, lambda e: e.scalar_tensor_tensor(out=qtl.ap, in0=qf.ap, scalar=128.0 ** -0.5, in1=e1.ap,
                                                                                   op0=ALU.mult, op1=ALU.mult), reads=[qf, e1], writes=[qtl]))
                    st_.append(lambda: fw.op(dve, lambda e: e.tensor_tensor(out=ktl.ap, in0=kf.ap, in1=e2.ap, op=ALU.mult), reads=[kf, e2], writes=[ktl]))

                    def transposes():
                        pb7 = B[7].ap.bitcast(BF16)
                        for ib in range(4):
                            fw.op(pe, lambda e: e.transpose(out=pb7[:, ib * 128:(ib + 1) * 128], in_=ktl.ap[:, ib * 128:(ib + 1) * 128],
                                                            identity=identb), reads=[ktl, constb], writes=[B[7]])
                        fw.op(act, lambda e: e.activation(out=ktok.ap, in_=pb7[:, 0:512].rearrange("p (i n) -> p i n", n=128),
                                                          func=AF.Identity), reads=[B[7]], writes=[ktok])
                    st_.append(transposes)
                    return st_

                def recur(g, steps):
                    hd, tt = divmod(g, 4)
                    P = PB[g % 2]
                    zs, qtl, ktl, ktok, vtok, hs = P["zs"], P["qtl"], P["ktl"], P["ktok"], P["vtok"], P["hs"]
                    kap = hs.ap[:, 48:56]
                    OTb = B[4 + g % 2]
                    per = -(-len(steps) // 8) if steps else 0
                    si = 0
                    if tt == 0:
                        fw.op(pool, lambda e: e.memset(Sst[0].ap, 0.0), writes=[Sst[0]])
                    fw.op(act, lambda e: e.activation(out=Sb[0].ap, in_=Sst[0].ap, func=AF.Identity, scale=kap[:, 0:1]),
                          reads=[Sst[0], hs], writes=[Sb[0]])
                    for c in range(8):
                        pr = (c % 2) * 64
                        ib = c // 2
                        cs = slice(c * 64, (c + 1) * 64)
                        aT, ac, up, tu = attnT[c % 2], aTc[c % 2], Up[c % 2], tmpU[c % 2]
                        s_old, s_new = Sst[c % 2], Sst[(c + 1) % 2]
                        atp = ATp[c % 4]
                        fw.op(pe, lambda e: e.matmul(atp.ap[pr:pr + 64, :], lhsT=ktl.ap[:, cs], rhs=qtl.ap[:, cs], start=True, stop=True),
                              reads=[ktl, qtl], writes=[atp])
                        fw.op(pe, lambda e: e.matmul(up.ap, lhsT=ktok.ap[pr:pr + 64, ib, :], rhs=vtok.ap[pr:pr + 64, ib, :],
                                                     start=True, stop=True), reads=[ktok, vtok], writes=[up])
                        fw.op(dve, lambda e: e.tensor_scalar(out=ac.ap[pr:pr + 64, :], in0=atp.ap[pr:pr + 64, :], scalar1=1e30, scalar2=-1e30,
                                                             op0=ALU.min, op1=ALU.max), reads=[atp], writes=[ac])
                        fw.op(dve, lambda e: e.tensor_tensor(out=aT.ap[pr:pr + 64, :], in0=ac.ap[pr:pr + 64, :], in1=maskH[pr:pr + 64, :],
                                                             op=ALU.mult), reads=[ac, constb], writes=[aT])
                        fw.op(dve, lambda e: e.scalar_tensor_tensor(out=s_new.ap, in0=s_old.ap, scalar=kap[:, c:c + 1], in1=up.ap,
                                                                    op0=ALU.mult, op1=ALU.add), reads=[s_old, hs, up], writes=[s_new])
                        if c < 7:
                            fw.op(act, lambda e: e.activation(out=Sb[(c + 1) % 2].ap, in_=s_new.ap, func=AF.Identity, scale=kap[:, c + 1:c + 2]),
                                  reads=[s_new, hs], writes=[Sb[(c + 1) % 2]])
                        for _ in range(per):
                            if si < len(steps):
                                steps[si]()
                                si += 1
                        fw.op(pe, lambda e: e.matmul(OTb.ap[:, cs], lhsT=vtok.ap[pr:pr + 64, ib, :], rhs=aT.ap[pr:pr + 64, :],
                                                     start=True, stop=False), reads=[vtok, aT], writes=[OTb])
                        fw.op(pe, lambda e: e.matmul(OTb.ap[:, cs], lhsT=Sb[c % 2].ap, rhs=qtl.ap[:, cs], start=False, stop=True),
                              reads=[Sb[c % 2], qtl], writes=[OTb])
                    while si < len(steps):
                        steps[si]()
                        si += 1
                    fw.op(act, lambda e: e.activation(out=o2.ap, in_=OTb.ap, func=AF.Square), reads=[OTb], writes=[o2])
                    fw.op(pe, lambda e: e.matmul(B[7].ap, lhsT=onesb, rhs=o2.ap, start=True, stop=True), reads=[constb, o2], writes=[B[7]])
                    fw.op(act, lambda e: e.activation(out=r1.ap, in_=B[7].ap, func=AF.Ln, scale=1.0 / 128, bias=1e-6), reads=[B[7]], writes=[r1])
                    fw.op(act, lambda e: e.activation(out=r1.ap, in_=r1.ap, func=AF.Exp, scale=-0.5), reads=[r1], writes=[r1])
                    fw.op(dve, lambda e: e.tensor_tensor(out=r2.ap, in0=OTb.ap, in1=r1.ap, op=ALU.mult), reads=[OTb, r1], writes=[r2])
                    fw.op(dve, lambda e: e.scalar_tensor_tensor(out=YT[0][tt].ap[:, hd, :], in0=r2.ap, scalar=nw, in1=zs.ap,
                                                                op0=ALU.mult, op1=ALU.mult), reads=[r2, vecs, zs], writes=[YT[0][tt]])

                for f_ in prep_steps(0):
                    f_()
                for g in range(16):
                    recur(g, prep_steps(g + 1) if g < 15 else [])

            with fw.scope():
                Ub = [fw.sbt(f"c_U{i}", [128, 30 + S], BF16) for i in range(2)]
                Dg = [fw.sbt(f"c_D{i}", [128, 31, 128], BF16) for i in range(2)]
                tg = [fw.sbt(f"c_tg{i}", [128, 512], F32) for i in range(2)]
                for i in range(2):
                    fw.op(pool, lambda e: e.memset(Ub[i].ap[:, 0:30], 0.0), writes=[Ub[i]])
                for cb in range(4):
                    wsl = acquire("conv")
                    Wc = wsl.ap[:, 0:4096].rearrange("p (kb n) -> p kb n", n=512)
                    ub = Ub[cb % 2]
                    dgc = Dg[cb % 2]
                    for k in range(31):
                        fw.op(dve, lambda e: e.tensor_scalar(out=dgc.ap[:, k, :], in0=identb,
                                                              scalar1=cwhT.ap[:, l * 124 + cb * 31 + k:l * 124 + cb * 31 + k + 1],
                                                              scalar2=None, op0=ALU.mult), reads=[constb, cwhT], writes=[dgc])
                    for tt in range(4):
                        h_ = HT[tt]
                        ba, bg = B[tt % 2], B[2 + tt % 2]
                        for (bk, c0) in ((ba, 0), (bg, 128)):
                            for kb in range(8):
                                fw.op(pe, lambda e: e.matmul(bk.ap, lhsT=Wc[:, kb, c0:c0 + 128], rhs=h_.ap[:, kb, :],
                                                             start=(kb == 0), stop=(kb == 7)), reads=[wsl, h_], writes=[bk])
                        t_ = tg[tt % 2]
                        fw.op(act, lambda e: e.activation(out=t_.ap, in_=bg.ap, func=AF.Tanh, scale=0.5), reads=[bg], writes=[t_])
                        fw.op(dve, lambda e: e.scalar_tensor_tensor(out=ub.ap[:, 30 + tt * 512:30 + (tt + 1) * 512], in0=t_.ap, scalar=1.0,
                                                                    in1=ba.ap, op0=ALU.add, op1=ALU.mult), reads=[t_, ba], writes=[ub])
                    for tt in range(4):
                        by = B[4 + tt % 2]
                        for k in range(31):
                            fw.op(pe, lambda e: e.matmul(by.ap, lhsT=dgc.ap[:, k, :], rhs=ub.ap[:, tt * 512 + k:tt * 512 + k + 512],
                                                         start=(k == 0), stop=(k == 30)), reads=[dgc, ub], writes=[by])
                        fw.op(dve, lambda e: e.tensor_scalar(out=YT[1][tt].ap[:, cb, :], in0=by.ap, scalar1=vcol(l, 161 + cb, 1), scalar2=None,
                                                             op0=ALU.add), reads=[by, vecs], writes=[YT[1][tt]])

            with fw.scope():
                ysq = [fw.sbt(f"c_ysq{i}", [128, 512], BF16) for i in range(4)]
                mean = fw.sbt("c_mean", [128, 512], F32)
                m2 = fw.sbt("c_m2", [128, 512], F32)
                rs = fw.sbt("c_rs", [128, 512], F32)
                zsc = [fw.sbt(f"c_zs{i}", [128, 512], BF16) for i in range(2)]
                dd = [fw.sbt(f"c_dd{i}", [128, 512], F32) for i in range(2)]
                tt_ = [fw.sbt(f"c_tt{i}", [128, 512], BF16) for i in range(2)]
                wsl = acquire("convz")
                Wz = wsl.ap[:, 0:4096].rearrange("p (kb n) -> p kb n", n=512)
                for tt in range(4):
                    y_ = YT[1][tt]
                    h_ = HT[tt]
                    for cb in range(4):
                        fw.op(act, lambda e: e.activation(out=ysq[cb].ap, in_=y_.ap[:, cb, :], func=AF.Square),
                              reads=[y_], writes=[ysq[cb]])
                    for cb in range(4):
                        fw.op(pe, lambda e: e.matmul(B[6].ap, lhsT=onesb, rhs=y_.ap[:, cb, :], start=(cb == 0), stop=(cb == 3)),
                              reads=[constb, y_], writes=[B[6]])
                    for cb in range(4):
                        fw.op(pe, lambda e: e.matmul(B[7].ap, lhsT=onesb, rhs=ysq[cb].ap, start=(cb == 0), stop=(cb == 3)),
                              reads=[constb, ysq[cb]], writes=[B[7]])
                    fw.op(dve, lambda e: e.tensor_scalar(out=mean.ap, in0=B[6].ap, scalar1=1.0 / W, scalar2=None, op0=ALU.mult),
                          reads=[B[6]], writes=[mean])
                    fw.op(dve, lambda e: e.tensor_tensor(out=m2.ap, in0=mean.ap, in1=mean.ap, op=ALU.mult), reads=[mean], writes=[m2])
                    fw.op(dve, lambda e: e.scalar_tensor_tensor(out=rs.ap, in0=B[7].ap, scalar=1.0 / W, in1=m2.ap, op0=ALU.mult, op1=ALU.subtract),
                          reads=[B[7], m2], writes=[rs])
                    fw.op(act, lambda e: e.activation(out=rs.ap, in_=rs.ap, func=AF.Ln, bias=1e-6), reads=[rs], writes=[rs])
                    fw.op(act, lambda e: e.activation(out=rs.ap, in_=rs.ap, func=AF.Exp, scale=-0.5), reads=[rs], writes=[rs])
                    for cb in range(4):
                        bz = B[cb % 2]
                        for kb in range(8):
                            fw.op(pe, lambda e: e.matmul(bz.ap, lhsT=Wz[:, kb, cb * 128:(cb + 1) * 128], rhs=h_.ap[:, kb, :],
                                                         start=(kb == 0), stop=(kb == 7)), reads=[wsl, h_], writes=[bz])
                        z_ = zsc[cb % 2]
                        d_ = dd[cb % 2]
                        t_ = tt_[cb % 2]
                        fw.op(act, lambda e: e.activation(out=z_.ap, in_=bz.ap, func=AF.Silu), reads=[bz], writes=[z_])
                        fw.op(dve, lambda e: e.tensor_tensor(out=d_.ap, in0=y_.ap[:, cb, :], in1=mean.ap, op=ALU.subtract),
                              reads=[y_, mean], writes=[d_])
                        fw.op(dve, lambda e: e.tensor_tensor(out=d_.ap, in0=d_.ap, in1=rs.ap, op=ALU.mult), reads=[d_, rs], writes=[d_])
                        fw.op(act, lambda e: e.activation(out=t_.ap, in_=d_.ap, func=AF.Silu, scale=vcol(l, 165 + cb, 1), bias=vcol(l, 169 + cb, 1)),
                              reads=[d_, vecs], writes=[t_])
                        fw.op(dve, lambda e: e.tensor_tensor(out=y_.ap[:, cb, :], in0=t_.ap, in1=z_.ap, op=ALU.mult),
                              reads=[t_, z_], writes=[y_])

            with fw.scope():
                qTm = [fw.sbt(f"a_qT{i}", [128, S], BF16) for i in range(2)]
                kT = fw.sbt("a_kT", [128, S], BF16)
                zsT = fw.sbt("a_zsT", [128, S], BF16)
                vtk = fw.sbt("a_vtok", [128, 16, 128], BF16)
                Eb = [fw.sbt(f"a_E{i}", [128, 512], F32) for i in range(2)]
                SPb = [fw.sbt(f"a_SP{i}", [128, 512], BF16) for i in range(3)]
                Rb = [fw.sbt(f"a_Rb{i}", [128, 512], BF16) for i in range(2)]
                wb_ = [fw.sbt(f"a_w{i}", [128, 512], BF16) for i in range(2)]
                fw.op(pool, lambda e: e.memset(qTm[0].ap[64:128, :], 0.0), writes=[qTm[0]])
                fw.op(pool, lambda e: e.memset(qTm[1].ap[0:64, :], 0.0), writes=[qTm[1]])
                for hp in range(4):
                    wsl = acquire("attn")
                    Wp = wsl.ap[:, 0:4096].rearrange("p (kb n) -> p kb n", n=512)
                    for tt in range(4):
                        h_ = HT[tt]
                        tok = slice(tt * 512, (tt + 1) * 512)
                        for (bk, c0) in ((B[2], 0), (B[3], 128)):
                            for kb in range(8):
                                fw.op(pe, lambda e: e.matmul(bk.ap, lhsT=Wp[:, kb, c0:c0 + 128], rhs=h_.ap[:, kb, :],
                                                             start=(kb == 0), stop=(kb == 7)), reads=[wsl, h_], writes=[bk])
                        for hh in range(2):
                            ph = hh * 64
                            fw.op(dve, lambda e: e.tensor_scalar(out=qTm[hh].ap[ph:ph + 64, tok], in0=B[2].ap[ph:ph + 64, :], scalar1=0.125,
                                                                 scalar2=None, op0=ALU.mult), reads=[B[2]], writes=[qTm[hh]])
                        fw.op(act, lambda e: e.activation(out=kT.ap[:, tok], in_=B[3].ap, func=AF.Identity), reads=[B[3]], writes=[kT])
                        for kb in range(8):
                            fw.op(pe, lambda e: e.matmul(B[2].ap, lhsT=Wp[:, kb, 384:512], rhs=h_.ap[:, kb, :],
                                                         start=(kb == 0), stop=(kb == 7)), reads=[wsl, h_], writes=[B[2]])
                        for ib in range(4):
                            for kb in range(8):
                                fw.op(pe, lambda e: e.matmul(B[3].ap[:, ib * 128:(ib + 1) * 128], lhsT=h_.ap[:, kb, ib * 128:(ib + 1) * 128],
                                                             rhs=Wp[:, kb, 256:384], start=(kb == 0), stop=(kb == 7)),
                                      reads=[wsl, h_], writes=[B[3]])
                        fw.op(act, lambda e: e.activation(out=zsT.ap[:, tok], in_=B[2].ap, func=AF.Silu), reads=[B[2]], writes=[zsT])
                        fw.op(dve, lambda e: e.tensor_copy(out=vtk.ap[:, tt * 4:(tt + 1) * 4, :], in_=B[3].ap.rearrange("p (i n) -> p i n", n=128)),
                              reads=[B[3]], writes=[vtk])
                    units = []
                    for qt in range(4):
                        for hh in range(2):
                            nkb = 4 * qt + 4
                            grp = qt * 2 + hh
                            for kb in range(nkb - 1, -1, -1):
                                dk = kb - 4 * qt
                                units.append(dict(qt=qt, hh=hh, kb=kb, grp=grp, gi=nkb - 1 - kb, first=(kb == nkb - 1), last=(kb == 0), dk=dk,
                                                  c0=(128 * dk if dk > 0 else 0)))
                    for ui, un in enumerate(units):
                        un["u"] = ui

                    def stage_a(un):
                        u, hh, kb, c0, dk = un["u"], un["hh"], un["kb"], un["c0"], un["dk"]
                        q0 = un["qt"] * 512
                        L, E_, SP_ = B[2 + u % 2], Eb[u % 2], SPb[u % 3]
                        fw.op(pe, lambda e: e.matmul(L.ap[:, c0:512], lhsT=kT.ap[:, kb * 128:(kb + 1) * 128],
                                                     rhs=qTm[hh].ap[:, q0 + c0:q0 + 512], start=True, stop=True),
                              reads=[kT, qTm[hh]], writes=[L])
                        fw.op(act, lambda e: e.activation(out=E_.ap[:, c0:512], in_=L.ap[:, c0:512], func=AF.Exp), reads=[L], writes=[E_])
                        fw.op(act, lambda e: e.activation(out=SP_.ap[:, c0:512], in_=E_.ap[:, c0:512], func=AF.Ln, bias=1.0),
                              reads=[E_], writes=[SP_])
                        if dk >= 0:
                            fw.op(dve, lambda e: e.tensor_tensor(out=SP_.ap[:, c0:512], in0=SP_.ap[:, c0:512], in1=maskA[dk][:, c0:512],
                                                                 op=ALU.mult), reads=[SP_, constb], writes=[SP_])
                        if un["first"] and c0 > 0:
                            fw.op(dve, lambda e: e.memset(SP_.ap[:, 0:c0], 0.0), writes=[SP_])

                    def stage_b(un):
                        u, hh, kb, c0, dk, first = un["u"], un["hh"], un["kb"], un["c0"], un["dk"], un["first"]
                        q0 = un["qt"] * 512
                        ARG, SP_, w_ = B[4 + u % 2], SPb[u % 3], wb_[u % 2]
                        rb_prev, rb_next = Rb[u % 2], Rb[(u + 1) % 2]
                        fw.op(pe, lambda e: e.matmul(ARG.ap[:, c0:512], lhsT=kT.ap[:, kb * 128:(kb + 1) * 128],
                                                     rhs=qTm[hh].ap[:, q0 + c0:q0 + 512], start=True, stop=False),
                              reads=[kT, qTm[hh]], writes=[ARG])
                        fw.op(pe, lambda e: e.matmul(ARG.ap[:, c0:512], lhsT=trineg, rhs=SP_.ap[:, c0:512], start=False, stop=first),
                              reads=[constb, SP_], writes=[ARG])
                        if not first:
                            fw.op(pe, lambda e: e.matmul(ARG.ap[:, c0:512], lhsT=onesneg, rhs=rb_prev.ap[:, c0:512], start=False, stop=True),
                                  reads=[constb, rb_prev], writes=[ARG])
                        if kb > 0:
                            c1 = 128 * (dk - 1) if dk > 1 else 0
                            RB = B[un["grp"] % 2]
                            ra = 0 if first else c0
                            fw.op(pe, lambda e: e.matmul(RB.ap[:, ra:512], lhsT=identb, rhs=SP_.ap[:, ra:512], start=first, stop=True,
                                                         skip_group_check=True), reads=[constb, SP_], writes=[RB])
                            fw.op(dve, lambda e: e.tensor_copy(out=rb_next.ap[:, c1:512], in_=RB.ap[:, c1:512]), reads=[RB], writes=[rb_next])
                        fw.op(act, lambda e: e.activation(out=w_.ap[:, c0:512], in_=ARG.ap[:, c0:512], func=AF.Exp), reads=[ARG], writes=[w_])
                        if dk >= 0:
                            fw.op(dve, lambda e: e.tensor_tensor(out=w_.ap[:, c0:512], in0=w_.ap[:, c0:512], in1=maskA[dk][:, c0:512], op=ALU.mult),
                                  reads=[w_, constb], writes=[w_])
                            if c0 > 0:
                                fw.op(dve, lambda e: e.memset(w_.ap[:, 0:c0], 0.0), writes=[w_])

                    def stage_c(un):
                        u, hh, kb, qt = un["u"], un["hh"], un["kb"], un["qt"]
                        ph = hh * 64
                        OT, w_ = B[6 + hh], wb_[u % 2]
                        fw.op(pe, lambda e: e.matmul(OT.ap, lhsT=vtk.ap[:, kb, :], rhs=w_.ap, start=un["first"], stop=un["last"]),
                              reads=[vtk, w_], writes=[OT])
                        if un["last"]:
                            fw.op(dve, lambda e: e.tensor_tensor(out=YT[2][qt].ap[ph:ph + 64, hp, :], in0=OT.ap[ph:ph + 64, :],
                                                                 in1=zsT.ap[ph:ph + 64, qt * 512:(qt + 1) * 512], op=ALU.mult),
                                  reads=[OT, zsT], writes=[YT[2][qt]])

                    nU = len(units)
                    for i in range(nU + 3):
                        if i < nU:
                            stage_a(units[i])
                        if 0 <= i - 2 < nU:
                            stage_b(units[i - 2])
                        if 0 <= i - 3 < nU:
                            stage_c(units[i - 3])

            if debug and l == n_layers - 1:
                with fw.scope():
                    dsem = fw.new_sem("dsem")
                    fw.dma(pool, dsem, [(dbg_h[:, kb, :], HTf[:, kb, :]) for kb in range(8)], reads=HT)
                    for n in range(3):
                        fw.dma(pool, dsem, [(dbg_y[n][:, wb, :], YTf[n][:, wb, :]) for wb in range(4)], reads=YT[n])

            with fw.scope():
                mg = fw.sbt("m_merged", [128, 8, 1024], BF16)
                sg = [fw.sbt(f"m_sg{i}", [128, 512], BF16) for i in range(3)]
                macc = fw.sbt("m_acc", [128, 512], F32)
                mt = fw.sbt("m_t", [128, 512], F32)
                gbc = fw.sbt("m_gbc", [128, D], F32)
                tmpO = [fw.sbt(f"m_tmpO{i}", [128, 512], F32) for i in range(2)]
                dgt = [fw.sbt(f"m_dg{i}", [128, 128], F32) for i in range(2)]
                for j in range(8):
                    d_ = dgt[j % 2]
                    bk = B[6 + (j // 4)]
                    fw.op(dve, lambda e: e.tensor_scalar(out=d_.ap, in0=identf.ap, scalar1=modT[l][:, 16 + j:17 + j], scalar2=None, op0=ALU.mult),
                          reads=[identf, small], writes=[d_])
                    fw.op(pe, lambda e: e.matmul(bk.ap[:, (j % 4) * 128:(j % 4 + 1) * 128], lhsT=onesf.ap, rhs=d_.ap, start=True, stop=True),
                          reads=[onesf, d_], writes=[bk])
                for jh in range(2):
                    fw.op(act, lambda e: e.activation(out=gbc.ap[:, jh * 512:(jh + 1) * 512], in_=B[6 + jh].ap, func=AF.Identity),
                          reads=[B[6 + jh]], writes=[gbc])
                for half in range(2):
                    for j in range(8):
                        wsl = acquire("merge")
                        Wg = wsl.ap[:, 0:3072].rearrange("p (kb n) -> p kb n", n=384)
                        Wb = wsl.ap[:, 3072:4608].rearrange("p (n wb d) -> p n wb d", n=3, wb=4)
                        for t2 in range(2):
                            tt = half * 2 + t2
                            h_ = HT[tt]
                            for n in range(3):
                                for kb in range(8):
                                    fw.op(pe, lambda e: e.matmul(B[n].ap, lhsT=Wg[:, kb, n * 128:(n + 1) * 128], rhs=h_.ap[:, kb, :],
                                                                 start=(kb == 0), stop=(kb == 7)), reads=[wsl, h_], writes=[B[n]])
                                fw.op(act, lambda e: e.activation(out=sg[n].ap, in_=B[n].ap, func=AF.Sigmoid), reads=[B[n]], writes=[sg[n]])
                                y_ = YT[n][tt]
                                for wb in range(4):
                                    fw.op(pe, lambda e: e.matmul(B[3 + n].ap, lhsT=Wb[:, n, wb, :], rhs=y_.ap[:, wb, :],
                                                                 start=(wb == 0), stop=(wb == 3)), reads=[wsl, y_], writes=[B[3 + n]])
                            fw.op(dve, lambda e: e.tensor_tensor(out=macc.ap, in0=B[3].ap, in1=sg[0].ap, op=ALU.mult), reads=[B[3], sg[0]], writes=[macc])
                            fw.op(dve, lambda e: e.tensor_tensor(out=mt.ap, in0=B[4].ap, in1=sg[1].ap, op=ALU.mult), reads=[B[4], sg[1]], writes=[mt])
                            fw.op(dve, lambda e: e.tensor_tensor(out=macc.ap, in0=macc.ap, in1=mt.ap, op=ALU.add), reads=[macc, mt], writes=[macc])
                            fw.op(dve, lambda e: e.tensor_tensor(out=mt.ap, in0=B[5].ap, in1=sg[2].ap, op=ALU.mult), reads=[B[5], sg[2]], writes=[mt])
                            fw.op(dve, lambda e: e.tensor_tensor(out=mg.ap[:, j, t2 * 512:(t2 + 1) * 512], in0=macc.ap, in1=mt.ap, op=ALU.add),
                                  reads=[macc, mt], writes=[mg])
                    u = 0
                    for cc in range(2):
                        wsl = acquire("outp")
                        Wo = wsl.ap[:, 0:4096].rearrange("p (kb n) -> p kb n", n=512)
                        for i8 in range(8):
                            i = half * 8 + i8
                            bk = B[6 + u % 2]
                            to = tmpO[u % 2]
                            u += 1
                            for kb in range(8):
                                fw.op(pe, lambda e: e.matmul(bk.ap, lhsT=mg.ap[:, kb, i8 * 128:(i8 + 1) * 128], rhs=Wo[:, kb, :],
                                                             start=(kb == 0), stop=(kb == 7)), reads=[mg, wsl], writes=[bk])
                            fw.op(dve, lambda e: e.tensor_tensor(out=to.ap, in0=bk.ap, in1=gbc.ap[:, cc * 512:(cc + 1) * 512], op=ALU.mult),
                                  reads=[bk, gbc], writes=[to])
                            fw.op(dve, lambda e: e.tensor_tensor(out=XR[i].ap[:, cc * 512:(cc + 1) * 512], in0=XR[i].ap[:, cc * 512:(cc + 1) * 512],
                                                                  in1=to.ap, op=ALU.add), reads=[XR[i], to], writes=[XR[i]])

        with fw.scope():
            osem = [fw.new_sem(f"osem{i}") for i in range(2)]
            if final_norm:
                fnw = fw.sbt("fnw", [128, D], F32)
                junk = fw.sbt("junk2", [128, D], BF16)
                ot = [fw.sbt(f"ot{i}", [128, D], F32) for i in range(2)]
                fw.dma(sp, csem[3], [(fnw.ap, fnw_d)], writes=[fnw])
                for i in range(16):
                    fw.op(act, lambda e: e.activation(out=junk.ap, in_=XR[i].ap, func=AF.Square, accum_out=ss[:, i:i + 1]),
                          reads=[XR[i]], writes=[junk, small])
                fw.op(dve, lambda e: e.tensor_scalar(out=ss, in0=ss, scalar1=1.0 / D, scalar2=1e-6, op0=ALU.mult, op1=ALU.add),
                      reads=[small], writes=[small])
                fw.op(pool, lambda e: e.tensor_tensor(out=rstd, in0=ss, in1=mhalf, op=ALU.pow), reads=[small], writes=[small])
                for i in range(16):
                    o_ = ot[i % 2]
                    fw.op(dve, lambda e: e.scalar_tensor_tensor(out=o_.ap, in0=XR[i].ap, scalar=rstd[:, i:i + 1], in1=fnw.ap,
                                                                op0=ALU.mult, op1=ALU.mult), reads=[XR[i], small, fnw], writes=[o_])
                    fw.dma(sp, osem[i % 2], [(out_d[i * 128:(i + 1) * 128, :], o_.ap)], reads=[o_])
                fw.barrier(ot)
            else:
                for i in range(16):
                    fw.dma(sp, osem[i % 2], [(out_d[i * 128:(i + 1) * 128, :], XR[i].ap)], reads=[XR[i]])
                fw.barrier(XR)
        stats = {e.name: (e.n_ins, e.n_wait) for e in fw.engines}
        print("instr/waits", stats, "sems", fw.n_sems)
    return nc


def _pp(v, nb):
    return np.ascontiguousarray(np.asarray(v, np.float32).reshape(nb, 128).T)


def _host_consts():
    identf = np.eye(128, dtype=np.float32)
    cb = np.zeros((128, NCB), np.float32)
    cb[:, 0:128] = identf
    j = np.arange(128)[:, None]
    s = np.arange(128)[None, :]
    cb[:, 128:256] = -(j >= s).astype(np.float32)
    cb[:, 256:384] = -1.0
    cb[:, 384:512] = 1.0
    t64 = np.arange(64)[None, :]
    cb[:, 512:576] = ((np.arange(128)[:, None] % 64) <= t64).astype(np.float32)
    t512 = np.arange(512)[None, :]
    for dk in range(4):
        cb[:, 576 + dk * 512:576 + (dk + 1) * 512] = ((np.arange(128)[:, None] + 128 * dk) < t512).astype(np.float32)
    return identf, cb


def _vecs(b, c, ada_b, norm_w, hgrn_lb, hgrn_norm_w, conv_w, conv_b, conv_ln_w, conv_ln_b):
    v = np.zeros((128, NV), np.float32)
    for l in range(2):
        o = l * LV
        v[:, o:o + 24] = _pp(ada_b[l], 24)
        v[:, o + 24:o + 32] = _pp(norm_w[l], 8)
        v[:, o + 32:o + 36] = _pp(hgrn_lb[l], 4)
        v[:, o + 36] = np.asarray(hgrn_norm_w[l], np.float32)
        v[:, o + 37:o + 161] = np.asarray(conv_w[l], np.float32).T.reshape(4, 128, 31).transpose(1, 0, 2).reshape(128, 124)
        v[:, o + 161:o + 165] = _pp(conv_b[l], 4)
        v[:, o + 165:o + 169] = _pp(conv_ln_w[l], 4)
        v[:, o + 169:o + 173] = _pp(conv_ln_b[l], 4)
    cT = _pp(c[b], 8)
    v[:, 2 * LV:2 * LV + 16] = np.repeat(cT[:, :, None], 2, axis=2).reshape(128, 16)
    return v


def _in_maps(x, c, ada_w, ada_b, norm_w, w_in, hgrn_lb, hgrn_norm_w, conv_w, conv_b, conv_ln_w, conv_ln_b,
             w_branch, w_out, final_norm_w, cores):
    identf, cb = _host_consts()
    f = lambda a: np.ascontiguousarray(np.asarray(a, np.float32))
    ada_w, w_in, w_branch, w_out = f(ada_w), f(w_in), f(w_branch), f(w_out)
    fnw = np.ascontiguousarray(np.broadcast_to(np.asarray(final_norm_w, np.float32)[None, :], (128, D)))
    x = np.asarray(x, np.float32)
    maps = []
    for b in cores:
        maps.append({
            "x": np.ascontiguousarray(x[b]),
            "vecs": _vecs(b, np.asarray(c, np.float32), ada_b, norm_w, hgrn_lb, hgrn_norm_w, conv_w, conv_b, conv_ln_w, conv_ln_b),
            "identf": identf, "constb": cb, "fnw": fnw,
            "ada_w": ada_w, "w_in": w_in, "w_branch": w_branch, "w_out": w_out,
        })
    return maps


def kernel(x, c, ada_w, ada_b, norm_w, w_in, hgrn_lb, hgrn_norm_w, conv_w, conv_b, conv_ln_w, conv_ln_b,
           w_branch, w_out, final_norm_w):
    nc = build(2, True, False)
    maps = _in_maps(x, c, ada_w, ada_b, norm_w, w_in, hgrn_lb, hgrn_norm_w, conv_w, conv_b, conv_ln_w, conv_ln_b,
                    w_branch, w_out, final_norm_w, list(range(8)))
    res = run_bass_kernel_spmd(nc, maps, core_ids=list(range(8)))
    return np.stack([np.asarray(r["out"], np.float32) for r in res.results], axis=0)
```

```python
import numpy as np
from contextlib import ExitStack, contextmanager
import concourse.bass as bass
import concourse.mybir as mybir
from concourse.bass_utils import run_bass_kernel_spmd

F32 = mybir.dt.float32
BF16 = mybir.dt.bfloat16
AF = mybir.ActivationFunctionType
ALU = mybir.AluOpType

S = 2048
D = 1024
W = 512
NCOL = 8704
SEM_LIMIT = 12000
LV = 173
NV = 2 * LV + 16
NCB = 4 * 128 + 64 + 4 * 512


class SemObj:
    __slots__ = ("sem", "count", "name")

    def __init__(self, sem, name):
        self.sem = sem
        self.count = 0
        self.name = name


class Engine:
    def __init__(self, fw, name, h, is_pe=False, compute=True):
        self.fw = fw
        self.name = name
        self.h = h
        self.is_pe = is_pe
        self.own = []
        self.cur = None
        self.seen = {}
        self.n_wait = 0
        self.n_ins = 0
        if compute:
            self._new_sem()

    def _new_sem(self):
        s = self.fw.new_sem(f"{self.name}_s{len(self.own)}")
        self.own.append(s)
        self.cur = s


class Tile:
    __slots__ = ("ap", "name", "last_write", "reads")

    def __init__(self, ap, name=""):
        self.ap = ap
        self.name = name
        self.last_write = None
        self.reads = {}


class FW:
    def __init__(self, nc, stack):
        self.nc = nc
        self.stack = stack
        self.stacks = [stack]
        self.scope_tiles = [[]]
        self.n_sems = 0
        self.pe = Engine(self, "pe", nc.tensor, is_pe=True)
        self.act = Engine(self, "act", nc.scalar)
        self.dve = Engine(self, "dve", nc.vector)
        self.pool = Engine(self, "pool", nc.gpsimd)
        self.sp = Engine(self, "sp", nc.sync, compute=False)
        self.engines = [self.pe, self.act, self.dve, self.pool, self.sp]

    def new_sem(self, name):
        self.n_sems += 1
        return SemObj(self.stack.enter_context(self.nc.semaphore(name)), name)

    def sbt(self, name, shape, dtype):
        self.n_sb = getattr(self, "n_sb", 0) + 1
        t = self.stacks[-1].enter_context(self.nc.sbuf_tensor(f"sb{self.n_sb}_{name}", list(shape), dtype))
        tl = Tile(t[:], name)
        self.scope_tiles[-1].append(tl)
        return tl

    def sub(self, ap, name=""):
        tl = Tile(ap, name)
        self.scope_tiles[-1].append(tl)
        return tl

    @contextmanager
    def scope(self):
        st = ExitStack()
        self.stacks.append(st)
        self.scope_tiles.append([])
        try:
            yield
        finally:
            tiles = self.scope_tiles.pop()
            self.barrier(tiles)
            self.stacks.pop()
            st.close()

    def barrier(self, tiles):
        for eng in self.engines:
            self._deps(eng, tiles, tiles)

    def _deps(self, eng, reads, writes):
        deps = {}

        def add(s, v, kind):
            if s in eng.own:
                if eng.is_pe or kind != "raw":
                    return
            if deps.get(s, 0) < v:
                deps[s] = v

        for t in reads:
            if t.last_write is not None:
                add(t.last_write[0], t.last_write[1], "raw")
        for t in writes:
            if t.last_write is not None:
                add(t.last_write[0], t.last_write[1], "waw")
            for s, v in t.reads.items():
                add(s, v, "war")
        for s, v in deps.items():
            if eng.seen.get(s, 0) >= v:
                continue
            eng.h.wait_ge(s.sem, v)
            eng.seen[s] = v
            eng.n_wait += 1

    def _mark(self, mark, reads, writes):
        for t in reads:
            if t.reads.get(mark[0], 0) < mark[1]:
                t.reads[mark[0]] = mark[1]
        for t in writes:
            t.last_write = mark
            t.reads = {}

    def op(self, eng, fn, reads=(), writes=()):
        self._deps(eng, reads, writes)
        if eng.cur.count >= SEM_LIMIT:
            eng._new_sem()
        ins = fn(eng.h)
        ins.then_inc(eng.cur.sem, 1)
        eng.cur.count += 1
        eng.n_ins += 1
        self._mark((eng.cur, eng.cur.count), reads, writes)
        return ins

    def dma(self, q, semobj, pairs, reads=(), writes=()):
        self._deps(q, reads, writes)
        for out, in_ in pairs:
            q.h.dma_start(out=out, in_=in_).then_inc(semobj.sem, 16)
            semobj.count += 16
            q.n_ins += 1
        self._mark((semobj, semobj.count), reads, writes)


def build(n_layers=2, final_norm=True, debug=False):
    nc = bass.Bass("TRN2", target_bir_lowering=False)
    x_d = nc.dram_tensor("x", [S, D], F32, kind="ExternalInput").ap()
    vecs_d = nc.dram_tensor("vecs", [128, NV], F32, kind="ExternalInput").ap()
    identf_d = nc.dram_tensor("identf", [128, 128], F32, kind="ExternalInput").ap()
    constb_d = nc.dram_tensor("constb", [128, NCB], F32, kind="ExternalInput").ap()
    fnw_d = nc.dram_tensor("fnw", [128, D], F32, kind="ExternalInput").ap()
    ada_w_d = nc.dram_tensor("ada_w", [2, D, 3 * D], F32, kind="ExternalInput").ap()
    w_in_d = nc.dram_tensor("w_in", [2, D, NCOL], F32, kind="ExternalInput").ap()
    w_br_d = nc.dram_tensor("w_branch", [2, 3, W, D], F32, kind="ExternalInput").ap()
    w_out_d = nc.dram_tensor("w_out", [2, D, D], F32, kind="ExternalInput").ap()
    out_d = nc.dram_tensor("out", [S, D], F32, kind="ExternalOutput").ap()
    if debug:
        dbg_h = nc.dram_tensor("dbg_h", [128, 8, S], F32, kind="ExternalOutput").ap()
        dbg_y = nc.dram_tensor("dbg_y", [3, 128, 4, S], F32, kind="ExternalOutput").ap()

    with ExitStack() as st:
        fw = FW(nc, st)
        pe, act, dve, pool, sp = fw.pe, fw.act, fw.dve, fw.pool, fw.sp

        XRf = st.enter_context(nc.sbuf_tensor("x_res", [128, 16, D], F32))
        XR = [fw.sub(XRf[:, i, :], f"xr{i}") for i in range(16)]
        HTf = st.enter_context(nc.sbuf_tensor("hT", [128, 8, S], BF16))
        HT = [fw.sub(HTf[:, :, tt * 512:(tt + 1) * 512], f"ht{tt}") for tt in range(4)]
        YTf = [st.enter_context(nc.sbuf_tensor(f"yT{n}", [128, 4, S], BF16)) for n in range(3)]
        YT = [[fw.sub(YTf[n][:, :, tt * 512:(tt + 1) * 512], f"yt{n}_{tt}") for tt in range(4)] for n in range(3)]
        WS = [fw.sbt(f"wslot{i}", [128, 4608], BF16) for i in range(2)]
        WSEM = [fw.new_sem(f"wsem{i}") for i in range(2)]
        vecs = fw.sbt("vecs", [128, NV], F32)
        identf = fw.sbt("identf", [128, 128], F32)
        onesf = fw.sbt("onesf", [128, 128], F32)
        constb = fw.sbt("constb", [128, NCB], BF16)
        identb = constb.ap[:, 0:128]
        trineg = constb.ap[:, 128:256]
        onesneg = constb.ap[:, 256:384]
        onesb = constb.ap[:, 384:512]
        maskH = constb.ap[:, 512:576]
        maskA = [constb.ap[:, 576 + dk * 512:576 + (dk + 1) * 512] for dk in range(4)]
        small = fw.sbt("small", [128, 256], F32)
        ss = small.ap[:, 0:16]
        rstd = small.ap[:, 16:32]
        mhalf = small.ap[:, 32:48]
        cact = small.ap[:, 48:64]
        modT = [small.ap[:, 64 + l * 24:64 + (l + 1) * 24] for l in range(2)]
        Avec = [small.ap[:, 112 + l * 8:112 + (l + 1) * 8] for l in range(2)]
        c1v = [small.ap[:, 128 + l * 4:128 + (l + 1) * 4] for l in range(2)]
        c0v = [small.ap[:, 136 + l * 4:136 + (l + 1) * 4] for l in range(2)]
        sm_tmp = small.ap[:, 144:176]
        cwhT = fw.sbt("cwh", [128, 2 * 124], F32)
        psall = st.enter_context(nc.psum_tensor("psall", [128, 8 * 512], F32))
        B = [fw.sub(psall[:, i * 512:(i + 1) * 512], f"bank{i}") for i in range(8)]
        XSEM = [fw.new_sem(f"xsem{i}") for i in range(16)]
        csem = [fw.new_sem(f"csem{i}") for i in range(4)]

        def vcol(l, off, n=1):
            return vecs.ap[:, l * LV + off:l * LV + off + n]

        fw.dma(sp, csem[0], [(vecs.ap, vecs_d)], writes=[vecs])
        fw.dma(sp, csem[1], [(identf.ap, identf_d)], writes=[identf])
        fw.dma(pool, csem[2], [(constb.ap[:, 0:1312], constb_d[:, 0:1312]), (constb.ap[:, 1312:NCB], constb_d[:, 1312:NCB])], writes=[constb])
        x_loads = [(lambda i=i: fw.dma(sp, XSEM[i], [(XR[i].ap, x_d[i * 128:(i + 1) * 128, :])], writes=[XR[i]])) for i in range(16)]
        fw.op(pool, lambda e: e.memset(onesf.ap, 1.0), writes=[onesf])
        fw.op(pool, lambda e: e.memset(mhalf, -0.5), writes=[small])

        jobs = []
        for l in range(n_layers):
            for hd in range(4):
                jobs.append(("hgrn", l, hd))
            for cb in range(4):
                jobs.append(("conv", l, cb))
            jobs.append(("convz", l, 0))
            for hp in range(4):
                jobs.append(("attn", l, hp))
            for half in range(2):
                for j in range(8):
                    jobs.append(("merge", l, j))
                for cc in range(2):
                    jobs.append(("outp", l, cc))
        issued = [0]

        def wsrc(l, c0, n):
            return w_in_d[l, :, c0:c0 + n].rearrange("(kb p) n -> p kb n", p=128)

        def issue(k):
            kind, l, idx = jobs[k]
            slot = WS[k % 2]
            v512 = slot.ap[:, 0:4096].rearrange("p (kb n) -> p kb n", n=512)
            pairs = []
            if kind == "hgrn":
                for b in range(4):
                    pairs.append((v512[:, :, b * 128:(b + 1) * 128], wsrc(l, b * 512 + idx * 128, 128)))
            elif kind == "conv":
                pairs.append((v512[:, :, 0:128], wsrc(l, 2048 + idx * 128, 128)))
                pairs.append((v512[:, :, 128:256], wsrc(l, 2560 + idx * 128, 128)))
            elif kind == "convz":
                pairs.append((v512, wsrc(l, 3072, 512)))
            elif kind == "attn":
                for b in range(4):
                    pairs.append((v512[:, :, b * 128:(b + 1) * 128], wsrc(l, 3584 + b * 512 + idx * 128, 128)))
            elif kind == "merge":
                vg = slot.ap[:, 0:3072].rearrange("p (kb n) -> p kb n", n=384)
                vb = slot.ap[:, 3072:4608].rearrange("p (n wb d) -> p n wb d", n=3, wb=4)
                for n in range(3):
                    pairs.append((vg[:, :, n * 128:(n + 1) * 128], wsrc(l, 5632 + n * 1024 + idx * 128, 128)))
                    pairs.append((vb[:, n, :, :],
                                  w_br_d[l, n, :, idx * 128:(idx + 1) * 128].rearrange("(wb p) d -> p wb d", p=128)))
            elif kind == "outp":
                pairs.append((v512, w_out_d[l, :, idx * 512:(idx + 1) * 512].rearrange("(kb p) n -> p kb n", p=128)))
            fw.dma(pool, WSEM[k % 2], pairs, writes=[slot])

        def acquire(expect):
            k = acquire.k
            assert jobs[k][0] == expect, (jobs[k], expect)
            while issued[0] <= min(k + 1, len(jobs) - 1):
                issue(issued[0])
                issued[0] += 1
            acquire.k += 1
            return WS[k % 2]

        acquire.k = 0

        cT2 = vecs.ap[:, 2 * LV:2 * LV + 16]
        fw.op(act, lambda e: e.activation(out=cact, in_=cT2, func=AF.Silu), reads=[vecs], writes=[small])
        def mod_steps(l, AW, awsem, bank):
            cact3 = cact.rearrange("p (k t) -> p k t", t=2)
            steps = []
            for jg in range(6):
                aw = AW[jg % 2]

                def ld(jg=jg, aw=aw):
                    fw.dma(sp, awsem[jg % 2],
                           [(aw.ap, ada_w_d[l, :, jg * 512:(jg + 1) * 512].rearrange("(kb p) n -> p kb n", p=128))],
                           writes=[aw])
                steps.append(ld)
                for jj in range(4):
                    def mm(jg=jg, jj=jj, aw=aw):
                        col = 2 * (jg * 4 + jj)
                        for kb in range(8):
                            fw.op(pe, lambda e: e.matmul(bank.ap[:, col:col + 2], lhsT=aw.ap[:, kb, jj * 128:(jj + 1) * 128],
                                                         rhs=cact3[:, kb, :], start=(kb == 0), stop=(kb == 7)),
                                  reads=[aw, small], writes=[bank])
                    steps.append(mm)
            return steps

        def mod_finish(l, bank):
            src = bank.ap[:, 0:48].rearrange("p (j t) -> p j t", t=2)[:, :, 0]
            fw.op(dve, lambda e: e.tensor_tensor(out=modT[l], in0=src, in1=vcol(l, 0, 24), op=ALU.add),
                  reads=[bank, vecs], writes=[small])
            fw.op(dve, lambda e: e.scalar_tensor_tensor(out=Avec[l], in0=modT[l][:, 8:16], scalar=1.0, in1=vcol(l, 24, 8),
                                                        op0=ALU.add, op1=ALU.mult), reads=[small, vecs], writes=[small])

        with fw.scope():
            AW = [fw.sbt(f"aw{i}", [128, 8, 512], F32) for i in range(2)]
            awsem = [fw.new_sem(f"awsem{i}") for i in range(2)]
            n_ld = 0
            for f_ in mod_steps(0, AW, awsem, B[0]):
                f_()
                if getattr(f_, "__name__", "") == "ld":
                    n_ld += 1
                    if n_ld >= 2:
                        for _ in range(4):
                            if x_loads:
                                x_loads.pop(0)()
            while x_loads:
                x_loads.pop(0)()
            mod_finish(0, B[0])
            a0 = vcol(0, 32, 4)
            a1 = vcol(1, 32, 4)
            t_mx, t_e0, t_e1, t_s, t_p0, t_p1, t_lo = [sm_tmp[:, 4 * i:4 * i + 4] for i in range(7)]
            fw.op(dve, lambda e: e.tensor_tensor(out=t_mx, in0=a0, in1=a1, op=ALU.max), reads=[vecs], writes=[small])
            fw.op(dve, lambda e: e.tensor_tensor(out=t_e0, in0=a0, in1=t_mx, op=ALU.subtract), reads=[vecs, small], writes=[small])
            fw.op(dve, lambda e: e.tensor_tensor(out=t_e1, in0=a1, in1=t_mx, op=ALU.subtract), reads=[vecs, small], writes=[small])
            fw.op(act, lambda e: e.activation(out=t_e0, in_=t_e0, func=AF.Exp), reads=[small], writes=[small])
            fw.op(act, lambda e: e.activation(out=t_e1, in_=t_e1, func=AF.Exp), reads=[small], writes=[small])
            fw.op(dve, lambda e: e.tensor_tensor(out=t_s, in0=t_e0, in1=t_e1, op=ALU.add), reads=[small], writes=[small])
            fw.op(dve, lambda e: e.reciprocal(out=t_s, in_=t_s), reads=[small], writes=[small])
            fw.op(dve, lambda e: e.tensor_tensor(out=t_p0, in0=t_e0, in1=t_s, op=ALU.mult), reads=[small], writes=[small])
            fw.op(dve, lambda e: e.tensor_tensor(out=t_p1, in0=t_e1, in1=t_s, op=ALU.mult), reads=[small], writes=[small])
            for l in range(2):
                if l == 0:
                    fw.op(dve, lambda e: e.tensor_tensor(out=t_lo, in0=t_p0, in1=t_p0, op=ALU.subtract), reads=[small], writes=[small])
                else:
                    fw.op(dve, lambda e: e.tensor_tensor(out=t_lo, in0=t_p0, in1=t_p1, op=ALU.add), reads=[small], writes=[small])
                    fw.op(dve, lambda e: e.tensor_tensor(out=t_lo, in0=t_lo, in1=t_p0, op=ALU.subtract), reads=[small], writes=[small])
                fw.op(dve, lambda e: e.tensor_scalar(out=c1v[l], in0=t_lo, scalar1=-0.5, scalar2=0.5, op0=ALU.mult, op1=ALU.add),
                      reads=[small], writes=[small])
                fw.op(dve, lambda e: e.tensor_scalar(out=c0v[l], in0=t_lo, scalar1=0.5, scalar2=0.5, op0=ALU.mult, op1=ALU.add),
                      reads=[small], writes=[small])
            for l in range(2):
                fw.op(dve, lambda e: e.tensor_scalar(out=cwhT.ap[:, l * 124:(l + 1) * 124], in0=vcol(l, 37, 124), scalar1=0.5,
                                                     scalar2=None, op0=ALU.mult), reads=[vecs], writes=[cwhT])

        for l in range(n_layers):
            with fw.scope():
                junk = fw.sbt("junk", [128, D], BF16)
                dg = [fw.sbt(f"dg{i}", [128, 128], F32) for i in range(2)]
                msteps = []
                if l + 1 < n_layers:
                    AW2 = [fw.sbt(f"aw2{i}", [128, 8, 512], F32) for i in range(2)]
                    awsem2 = [fw.new_sem(f"awsem2_{l}_{i}") for i in range(2)]
                    msteps = mod_steps(l + 1, AW2, awsem2, B[4])
                mi = 0
                for i in range(16):
                    fw.op(act, lambda e: e.activation(out=junk.ap, in_=XR[i].ap, func=AF.Square, accum_out=ss[:, i:i + 1]),
                          reads=[XR[i]], writes=[junk, small])
                fw.op(dve, lambda e: e.tensor_scalar(out=ss, in0=ss, scalar1=1.0 / D, scalar2=1e-6, op0=ALU.mult, op1=ALU.add),
                      reads=[small], writes=[small])
                fw.op(pool, lambda e: e.tensor_tensor(out=rstd, in0=ss, in1=mhalf, op=ALU.pow), reads=[small], writes=[small])
                for i in range(16):
                    d_ = dg[i % 2]
                    fw.op(dve, lambda e: e.tensor_scalar(out=d_.ap, in0=identf.ap, scalar1=rstd[:, i:i + 1], scalar2=None, op0=ALU.mult),
                          reads=[identf, small], writes=[d_])
                    for jh in range(2):
                        bk = B[(2 * i + jh) % 4]
                        for jj in range(4):
                            j = jh * 4 + jj
                            fw.op(pe, lambda e: e.matmul(bk.ap[:, jj * 128:(jj + 1) * 128], lhsT=XR[i].ap[:, j * 128:(j + 1) * 128],
                                                         rhs=d_.ap, start=True, stop=True), reads=[XR[i], d_], writes=[bk])
                        for jj in range(4):
                            j = jh * 4 + jj
                            o_ap = HTf[:, j, i * 128:(i + 1) * 128]
                            i_ap = bk.ap[:, jj * 128:(jj + 1) * 128]
                            if jj % 2 == 0:
                                fw.op(act, lambda e: e.activation(out=o_ap, in_=i_ap, func=AF.Identity, scale=Avec[l][:, j:j + 1],
                                                                  bias=modT[l][:, j:j + 1]), reads=[bk, small], writes=[HT[i // 4]])
                            else:
                                fw.op(dve, lambda e: e.tensor_scalar(out=o_ap, in0=i_ap, scalar1=Avec[l][:, j:j + 1],
                                                                     scalar2=modT[l][:, j:j + 1], op0=ALU.mult, op1=ALU.add),
                                      reads=[bk, small], writes=[HT[i // 4]])
                    for _ in range(2):
                        if mi < len(msteps):
                            msteps[mi]()
                            mi += 1
                while mi < len(msteps):
                    msteps[mi]()
                    mi += 1
                if l + 1 < n_layers:
                    mod_finish(l + 1, B[4])

            with fw.scope():
                qf = fw.sbt("h_qf", [128, 512], F32)
                tf = fw.sbt("h_tf", [128, 512], F32)
                kf = fw.sbt("h_kf", [128, 512], F32)
                bb = fw.sbt("h_bb", [128, 512], F32)
                e1 = fw.sbt("h_e1", [128, 512], F32)
                e2 = fw.sbt("h_e2", [128, 512], F32)
                r1 = fw.sbt("h_r1", [128, 512], F32)
                r2 = fw.sbt("h_r2", [128, 512], F32)
                o2 = fw.sbt("h_o2", [128, 512], BF16)
                ones64 = fw.sbt("h_ones64", [128, 64], F32)
                PB = []
                for i in range(2):
                    PB.append(dict(
                        zs=fw.sbt(f"h_zs{i}", [128, 512], BF16), qtl=fw.sbt(f"h_qtl{i}", [128, 512], BF16),
                        ktl=fw.sbt(f"h_ktl{i}", [128, 512], BF16), ktok=fw.sbt(f"h_ktok{i}", [128, 4, 128], BF16),
                        vtok=fw.sbt(f"h_vtok{i}", [128, 4, 128], BF16), hs=fw.sbt(f"h_small{i}", [128, 56], F32)))
                attnT = [fw.sbt(f"h_attnT{i}", [128, 64], BF16) for i in range(2)]
                aTc = [fw.sbt(f"h_aTc{i}", [128, 64], F32) for i in range(2)]
                Sst = [fw.sbt(f"h_S{i}", [128, 128], F32) for i in range(2)]
                Sb = [fw.sbt(f"h_Sb{i}", [128, 128], BF16) for i in range(2)]
                tmpU = [fw.sbt(f"h_tmpU{i}", [128, 128], F32) for i in range(2)]
                Up = [fw.sub(B[6].ap[:, c * 128:(c + 1) * 128], f"up{c}") for c in range(2)]
                ATp = [fw.sub(B[6].ap[:, 256 + c * 64:256 + (c + 1) * 64], f"atp{c}") for c in range(4)]
                fw.barrier([B[6]])
                fw.op(pool, lambda e: e.memset(ones64.ap, 1.0), writes=[ones64])
                nw = vcol(l, 36, 1)
                wcur = {}

                def prep_steps(g):
                    hd, tt = divmod(g, 4)
                    P = PB[g % 2]
                    zs, qtl, ktl, ktok, vtok, hs = P["zs"], P["qtl"], P["ktl"], P["ktok"], P["vtok"], P["hs"]
                    bm, bl, dlm = hs.ap[:, 0:8], hs.ap[:, 8:16], hs.ap[:, 16:24]
                    h_ = HT[tt]
                    st_ = []

                    def s_acq():
                        if tt == 0:
                            wcur[hd] = acquire("hgrn")
                    st_.append(s_acq)

                    def mm_block(bk, c0):
                        def f():
                            wsl = wcur[hd]
                            Wh = wsl.ap[:, 0:4096].rearrange("p (kb n) -> p kb n", n=512)
                            for kb in range(8):
                                fw.op(pe, lambda e: e.matmul(bk.ap, lhsT=Wh[:, kb, c0:c0 + 128], rhs=h_.ap[:, kb, :],
                                                             start=(kb == 0), stop=(kb == 7)), reads=[wsl, h_], writes=[bk])
                        return f
                    def mm_v(ib):
                        def f():
                            wsl = wcur[hd]
                            Wh = wsl.ap[:, 0:4096].rearrange("p (kb n) -> p kb n", n=512)
                            for kb in range(8):
                                fw.op(pe, lambda e: e.matmul(B[3].ap[:, ib * 128:(ib + 1) * 128], lhsT=h_.ap[:, kb, ib * 128:(ib + 1) * 128],
                                                             rhs=Wh[:, kb, 256:384], start=(kb == 0), stop=(kb == 7)),
                                      reads=[wsl, h_], writes=[B[3]])
                        return f

                    def scans(c0_):
                        def f():
                            for c in range(c0_, c0_ + 4):
                                fw.op(dve, lambda e: e.tensor_tensor_scan(out=bb.ap[:, c * 64:(c + 1) * 64], data0=ones64.ap,
                                                                          data1=e1.ap[:, c * 64:(c + 1) * 64], initial=0.0,
                                                                          op0=ALU.mult, op1=ALU.add), reads=[ones64, e1], writes=[bb])
                        return f
                    bb3 = bb.ap.rearrange("p (c l) -> p c l", l=64)

                    def smalls():
                        fw.op(dve, lambda e: e.tensor_copy(out=bm, in_=bb3[:, :, 31]), reads=[bb], writes=[hs])
                        fw.op(dve, lambda e: e.tensor_copy(out=bl, in_=bb3[:, :, 63]), reads=[bb], writes=[hs])
                        fw.op(dve, lambda e: e.tensor_tensor(out=dlm, in0=bl, in1=bm, op=ALU.subtract), reads=[hs], writes=[hs])

                    def kappa():
                        hprev = PB[(g + 1) % 2]["hs"]
                        fw.op(dve, lambda e: e.tensor_tensor(out=hs.ap[:, 49:56], in0=hs.ap[:, 25:32], in1=hs.ap[:, 40:47], op=ALU.mult),
                              reads=[hs], writes=[hs])
                        if tt == 0:
                            fw.op(dve, lambda e: e.tensor_copy(out=hs.ap[:, 48:49], in_=hs.ap[:, 24:25]), reads=[hs], writes=[hs])
                        else:
                            fw.op(dve, lambda e: e.tensor_tensor(out=hs.ap[:, 48:49], in0=hs.ap[:, 24:25], in1=hprev.ap[:, 47:48], op=ALU.mult),
                                  reads=[hs, hprev], writes=[hs])

                    st_.append(mm_block(B[1], 128))
                    st_.append(lambda: fw.op(act, lambda e: e.activation(out=tf.ap, in_=B[1].ap, func=AF.Tanh, scale=0.5), reads=[B[1]], writes=[tf]))
                    st_.append(lambda: fw.op(dve, lambda e: e.tensor_scalar(out=tf.ap, in0=tf.ap, scalar1=c1v[l][:, hd:hd + 1],
                                                                            scalar2=c0v[l][:, hd:hd + 1], op0=ALU.mult, op1=ALU.add),
                                             reads=[tf, small], writes=[tf]))
                    st_.append(mm_block(B[0], 0))
                    st_.append(lambda: fw.op(act, lambda e: e.activation(out=qf.ap, in_=B[0].ap, func=AF.Silu), reads=[B[0]], writes=[qf]))
                    st_.append(lambda: fw.op(act, lambda e: e.activation(out=kf.ap, in_=tf.ap, func=AF.Identity, scale=-1.0, bias=1.0),
                                             reads=[tf], writes=[kf]))
                    st_.append(lambda: fw.op(dve, lambda e: e.tensor_scalar_max(out=bb.ap, in0=tf.ap, scalar1=1e-30), reads=[tf], writes=[bb]))
                    st_.append(mm_block(B[2], 384))
                    st_.append(lambda: fw.op(act, lambda e: e.activation(out=zs.ap, in_=B[2].ap, func=AF.Silu), reads=[B[2]], writes=[zs]))
                    st_.append(lambda: fw.op(act, lambda e: e.activation(out=e1.ap, in_=bb.ap, func=AF.Ln), reads=[bb], writes=[e1]))
                    st_.append(mm_v(0))
                    st_.append(scans(0))
                    st_.append(mm_v(1))
                    st_.append(scans(4))
                    st_.append(mm_v(2))
                    st_.append(smalls)
                    st_.append(mm_v(3))
                    st_.append(lambda: fw.op(dve, lambda e: e.tensor_tensor(out=tf.ap.rearrange("p (c l) -> p c l", l=64), in0=bb3,
                                                                            in1=bm.unsqueeze(2).to_broadcast([128, 8, 64]), op=ALU.subtract),
                                             reads=[bb, hs], writes=[tf]))
                    st_.append(lambda: fw.op(act, lambda e: e.activation(out=vtok.ap, in_=B[3].ap.rearrange("p (i n) -> p i n", n=128),
                                                                         func=AF.Identity), reads=[B[3]], writes=[vtok]))
                    st_.append(lambda: fw.op(dve, lambda e: e.tensor_scalar(out=tf.ap, in0=tf.ap, scalar1=80.0, scalar2=-80.0, op0=ALU.min, op1=ALU.max),
                                             reads=[tf], writes=[tf]))
                    st_.append(lambda: fw.op(act, lambda e: e.activation(out=e1.ap, in_=tf.ap, func=AF.Exp), reads=[tf], writes=[e1]))
                    st_.append(lambda: fw.op(act, lambda e: e.activation(out=e2.ap, in_=tf.ap, func=AF.Exp, scale=-1.0), reads=[tf], writes=[e2]))
                    st_.append(lambda: fw.op(act, lambda e: e.activation(out=hs.ap[:, 24:48], in_=hs.ap[:, 0:24], func=AF.Exp), reads=[hs], writes=[hs]))
                    st_.append(kappa)
                    st_.append(lambda: fw.op(dve, lambda e: e.scalar_tensor_tensor(out=qtl.ap, in0=qf.ap, scalar=128.0 ** -0.5, in1=e1.ap,
                                                                                   op0=ALU.mult, op1=ALU.mult), reads=[qf, e1], writes=[qtl]))
                    st_.append(lambda: fw.op(dve, lambda e: e.tensor_tensor(out=ktl.ap, in0=kf.ap, in1=e2.ap, op=ALU.mult), reads=[kf, e2], writes=[ktl]))

                    def transposes():
                        pb7 = B[7].ap.bitcast(BF16)
                        for ib in range(4):
                            fw.op(pe, lambda e: e.transpose(out=pb7[:, ib * 128:(ib + 1) * 128], in_=ktl.ap[:, ib * 128:(ib + 1) * 128],
                                                            identity=identb), reads=[ktl, constb], writes=[B[7]])
                        fw.op(act, lambda e: e.activation(out=ktok.ap, in_=pb7[:, 0:512].rearrange("p (i n) -> p i n", n=128),
                                                          func=AF.Identity), reads=[B[7]], writes=[ktok])
                    st_.append(transposes)
                    return st_

                def recur(g, steps):
                    hd, tt = divmod(g, 4)
                    P = PB[g % 2]
                    zs, qtl, ktl, ktok, vtok, hs = P["zs"], P["qtl"], P["ktl"], P["ktok"], P["vtok"], P["hs"]
                    kap = hs.ap[:, 48:56]
                    OTb = B[4 + g % 2]
                    per = -(-len(steps) // 8) if steps else 0
                    si = 0
                    if tt == 0:
                        fw.op(pool, lambda e: e.memset(Sst[0].ap, 0.0), writes=[Sst[0]])
                    fw.op(act, lambda e: e.activation(out=Sb[0].ap, in_=Sst[0].ap, func=AF.Identity, scale=kap[:, 0:1]),
                          reads=[Sst[0], hs], writes=[Sb[0]])
                    for c in range(8):
                        pr = (c % 2) * 64
                        ib = c // 2
                        cs = slice(c * 64, (c + 1) * 64)
                        aT, ac, up, tu = attnT[c % 2], aTc[c % 2], Up[c % 2], tmpU[c % 2]
                        s_old, s_new = Sst[c % 2], Sst[(c + 1) % 2]
                        atp = ATp[c % 4]
                        fw.op(pe, lambda e: e.matmul(atp.ap[pr:pr + 64, :], lhsT=ktl.ap[:, cs], rhs=qtl.ap[:, cs], start=True, stop=True),
                              reads=[ktl, qtl], writes=[atp])
                        fw.op(pe, lambda e: e.matmul(up.ap, lhsT=ktok.ap[pr:pr + 64, ib, :], rhs=vtok.ap[pr:pr + 64, ib, :],
                                                     start=True, stop=True), reads=[ktok, vtok], writes=[up])
                        fw.op(dve, lambda e: e.tensor_scalar(out=ac.ap[pr:pr + 64, :], in0=atp.ap[pr:pr + 64, :], scalar1=1e30, scalar2=-1e30,
                                                             op0=ALU.min, op1=ALU.max), reads=[atp], writes=[ac])
                        fw.op(dve, lambda e: e.tensor_tensor(out=aT.ap[pr:pr + 64, :], in0=ac.ap[pr:pr + 64, :], in1=maskH[pr:pr + 64, :],
                                                             op=ALU.mult), reads=[ac, constb], writes=[aT])
                        fw.op(dve, lambda e: e.scalar_tensor_tensor(out=s_new.ap, in0=s_old.ap, scalar=kap[:, c:c + 1], in1=up.ap,
                                                                    op0=ALU.mult, op1=ALU.add), reads=[s_old, hs, up], writes=[s_new])
                        if c < 7:
                            fw.op(act, lambda e: e.activation(out=Sb[(c + 1) % 2].ap, in_=s_new.ap, func=AF.Identity, scale=kap[:, c + 1:c + 2]),
                                  reads=[s_new, hs], writes=[Sb[(c + 1) % 2]])
                        for _ in range(per):
                            if si < len(steps):
                                steps[si]()
                                si += 1
                        fw.op(pe, lambda e: e.matmul(OTb.ap[:, cs], lhsT=vtok.ap[pr:pr + 64, ib, :], rhs=aT.ap[pr:pr + 64, :],
                                                     start=True, stop=False), reads=[vtok, aT], writes=[OTb])
                        fw.op(pe, lambda e: e.matmul(OTb.ap[:, cs], lhsT=Sb[c % 2].ap, rhs=qtl.ap[:, cs], start=False, stop=True),
                              reads=[Sb[c % 2], qtl], writes=[OTb])
                    while si < len(steps):
                        steps[si]()
                        si += 1
                    fw.op(act, lambda e: e.activation(out=o2.ap, in_=OTb.ap, func=AF.Square), reads=[OTb], writes=[o2])
                    fw.op(pe, lambda e: e.matmul(B[7].ap, lhsT=onesb, rhs=o2.ap, start=True, stop=True), reads=[constb, o2], writes=[B[7]])
                    fw.op(act, lambda e: e.activation(out=r1.ap, in_=B[7].ap, func=AF.Ln, scale=1.0 / 128, bias=1e-6), reads=[B[7]], writes=[r1])
                    fw.op(act, lambda e: e.activation(out=r1.ap, in_=r1.ap, func=AF.Exp, scale=-0.5), reads=[r1], writes=[r1])
                    fw.op(dve, lambda e: e.tensor_tensor(out=r2.ap, in0=OTb.ap, in1=r1.ap, op=ALU.mult), reads=[OTb, r1], writes=[r2])
                    fw.op(dve, lambda e: e.scalar_tensor_tensor(out=YT[0][tt].ap[:, hd, :], in0=r2.ap, scalar=nw, in1=zs.ap,
                                                                op0=ALU.mult, op1=ALU.mult), reads=[r2, vecs, zs], writes=[YT[0][tt]])

                for f_ in prep_steps(0):
                    f_()
                for g in range(16):
                    recur(g, prep_steps(g + 1) if g < 15 else [])

            with fw.scope():
                Ub = [fw.sbt(f"c_U{i}", [128, 30 + S], BF16) for i in range(2)]
                Dg = [fw.sbt(f"c_D{i}", [128, 31, 128], BF16) for i in range(2)]
                tg = [fw.sbt(f"c_tg{i}", [128, 512], F32) for i in range(2)]
                for i in range(2):
                    fw.op(pool, lambda e: e.memset(Ub[i].ap[:, 0:30], 0.0), writes=[Ub[i]])
                for cb in range(4):
                    wsl = acquire("conv")
                    Wc = wsl.ap[:, 0:4096].rearrange("p (kb n) -> p kb n", n=512)
                    ub = Ub[cb % 2]
                    dgc = Dg[cb % 2]
                    for k in range(31):
                        fw.op(dve, lambda e: e.tensor_scalar(out=dgc.ap[:, k, :], in0=identb,
                                                              scalar1=cwhT.ap[:, l * 124 + cb * 31 + k:l * 124 + cb * 31 + k + 1],
                                                              scalar2=None, op0=ALU.mult), reads=[constb, cwhT], writes=[dgc])
                    for tt in range(4):
                        h_ = HT[tt]
                        ba, bg = B[tt % 2], B[2 + tt % 2]
                        for (bk, c0) in ((ba, 0), (bg, 128)):
                            for kb in range(8):
                                fw.op(pe, lambda e: e.matmul(bk.ap, lhsT=Wc[:, kb, c0:c0 + 128], rhs=h_.ap[:, kb, :],
                                                             start=(kb == 0), stop=(kb == 7)), reads=[wsl, h_], writes=[bk])
                        t_ = tg[tt % 2]
                        fw.op(act, lambda e: e.activation(out=t_.ap, in_=bg.ap, func=AF.Tanh, scale=0.5), reads=[bg], writes=[t_])
                        fw.op(dve, lambda e: e.scalar_tensor_tensor(out=ub.ap[:, 30 + tt * 512:30 + (tt + 1) * 512], in0=t_.ap, scalar=1.0,
                                                                    in1=ba.ap, op0=ALU.add, op1=ALU.mult), reads=[t_, ba], writes=[ub])
                    for tt in range(4):
                        by = B[4 + tt % 2]
                        for k in range(31):
                            fw.op(pe, lambda e: e.matmul(by.ap, lhsT=dgc.ap[:, k, :], rhs=ub.ap[:, tt * 512 + k:tt * 512 + k + 512],
                                                         start=(k == 0), stop=(k == 30)), reads=[dgc, ub], writes=[by])
                        fw.op(dve, lambda e: e.tensor_scalar(out=YT[1][tt].ap[:, cb, :], in0=by.ap, scalar1=vcol(l, 161 + cb, 1), scalar2=None,
                                                             op0=ALU.add), reads=[by, vecs], writes=[YT[1][tt]])

            with fw.scope():
                ysq = [fw.sbt(f"c_ysq{i}", [128, 512], BF16) for i in range(4)]
                mean = fw.sbt("c_mean", [128, 512], F32)
                m2 = fw.sbt("c_m2", [128, 512], F32)
                rs = fw.sbt("c_rs", [128, 512], F32)
                zsc = [fw.sbt(f"c_zs{i}", [128, 512], BF16) for i in range(2)]
                dd = [fw.sbt(f"c_dd{i}", [128, 512], F32) for i in range(2)]
                tt_ = [fw.sbt(f"c_tt{i}", [128, 512], BF16) for i in range(2)]
                wsl = acquire("convz")
                Wz = wsl.ap[:, 0:4096].rearrange("p (kb n) -> p kb n", n=512)
                for tt in range(4):
                    y_ = YT[1][tt]
                    h_ = HT[tt]
                    for cb in range(4):
                        fw.op(act, lambda e: e.activation(out=ysq[cb].ap, in_=y_.ap[:, cb, :], func=AF.Square),
                              reads=[y_], writes=[ysq[cb]])
                    for cb in range(4):
                        fw.op(pe, lambda e: e.matmul(B[6].ap, lhsT=onesb, rhs=y_.ap[:, cb, :], start=(cb == 0), stop=(cb == 3)),
                              reads=[constb, y_], writes=[B[6]])
                    for cb in range(4):
                        fw.op(pe, lambda e: e.matmul(B[7].ap, lhsT=onesb, rhs=ysq[cb].ap, start=(cb == 0), stop=(cb == 3)),
                              reads=[constb, ysq[cb]], writes=[B[7]])
                    fw.op(dve, lambda e: e.tensor_scalar(out=mean.ap, in0=B[6].ap, scalar1=1.0 / W, scalar2=None, op0=ALU.mult),
                          reads=[B[6]], writes=[mean])
                    fw.op(dve, lambda e: e.tensor_tensor(out=m2.ap, in0=mean.ap, in1=mean.ap, op=ALU.mult), reads=[mean], writes=[m2])
                    fw.op(dve, lambda e: e.scalar_tensor_tensor(out=rs.ap, in0=B[7].ap, scalar=1.0 / W, in1=m2.ap, op0=ALU.mult, op1=ALU.subtract),
                          reads=[B[7], m2], writes=[rs])
                    fw.op(act, lambda e: e.activation(out=rs.ap, in_=rs.ap, func=AF.Ln, bias=1e-6), reads=[rs], writes=[rs])
                    fw.op(act, lambda e: e.activation(out=rs.ap, in_=rs.ap, func=AF.Exp, scale=-0.5), reads=[rs], writes=[rs])
                    for cb in range(4):
                        bz = B[cb % 2]
                        for kb in range(8):
                            fw.op(pe, lambda e: e.matmul(bz.ap, lhsT=Wz[:, kb, cb * 128:(cb + 1) * 128], rhs=h_.ap[:, kb, :],
                                                         start=(kb == 0), stop=(kb == 7)), reads=[wsl, h_], writes=[bz])
                        z_ = zsc[cb % 2]
                        d_ = dd[cb % 2]
                        t_ = tt_[cb % 2]
                        fw.op(act, lambda e: e.activation(out=z_.ap, in_=bz.ap, func=AF.Silu), reads=[bz], writes=[z_])
                        fw.op(dve, lambda e: e.tensor_tensor(out=d_.ap, in0=y_.ap[:, cb, :], in1=mean.ap, op=ALU.subtract),
                              reads=[y_, mean], writes=[d_])
                        fw.op(dve, lambda e: e.tensor_tensor(out=d_.ap, in0=d_.ap, in1=rs.ap, op=ALU.mult), reads=[d_, rs], writes=[d_])
                        fw.op(act, lambda e: e.activation(out=t_.ap, in_=d_.ap, func=AF.Silu, scale=vcol(l, 165 + cb, 1), bias=vcol(l, 169 + cb, 1)),
                              reads=[d_, vecs], writes=[t_])
                        fw.op(dve, lambda e: e.tensor_tensor(out=y_.ap[:, cb, :], in0=t_.ap, in1=z_.ap, op=ALU.mult),
                              reads=[t_, z_], writes=[y_])

            with fw.scope():
                qTm = [fw.sbt(f"a_qT{i}", [128, S], BF16) for i in range(2)]
                kT = fw.sbt("a_kT", [128, S], BF16)
                zsT = fw.sbt("a_zsT", [128, S], BF16)
                vtk = fw.sbt("a_vtok", [128, 16, 128], BF16)
                Eb = [fw.sbt(f"a_E{i}", [128, 512], F32) for i in range(2)]
                SPb = [fw.sbt(f"a_SP{i}", [128, 512], BF16) for i in range(3)]
                Rb = [fw.sbt(f"a_Rb{i}", [128, 512], BF16) for i in range(2)]
                wb_ = [fw.sbt(f"a_w{i}", [128, 512], BF16) for i in range(2)]
                fw.op(pool, lambda e: e.memset(qTm[0].ap[64:128, :], 0.0), writes=[qTm[0]])
                fw.op(pool, lambda e: e.memset(qTm[1].ap[0:64, :], 0.0), writes=[qTm[1]])
                for hp in range(4):
                    wsl = acquire("attn")
                    Wp = wsl.ap[:, 0:4096].rearrange("p (kb n) -> p kb n", n=512)
                    for tt in range(4):
                        h_ = HT[tt]
                        tok = slice(tt * 512, (tt + 1) * 512)
                        for (bk, c0) in ((B[2], 0), (B[3], 128)):
                            for kb in range(8):
                                fw.op(pe, lambda e: e.matmul(bk.ap, lhsT=Wp[:, kb, c0:c0 + 128], rhs=h_.ap[:, kb, :],
                                                             start=(kb == 0), stop=(kb == 7)), reads=[wsl, h_], writes=[bk])
                        for hh in range(2):
                            ph = hh * 64
                            fw.op(dve, lambda e: e.tensor_scalar(out=qTm[hh].ap[ph:ph + 64, tok], in0=B[2].ap[ph:ph + 64, :], scalar1=0.125,
                                                                 scalar2=None, op0=ALU.mult), reads=[B[2]], writes=[qTm[hh]])
                        fw.op(act, lambda e: e.activation(out=kT.ap[:, tok], in_=B[3].ap, func=AF.Identity), reads=[B[3]], writes=[kT])
                        for kb in range(8):
                            fw.op(pe, lambda e: e.matmul(B[2].ap, lhsT=Wp[:, kb, 384:512], rhs=h_.ap[:, kb, :],
                                                         start=(kb == 0), stop=(kb == 7)), reads=[wsl, h_], writes=[B[2]])
                        for ib in range(4):
                            for kb in range(8):
                                fw.op(pe, lambda e: e.matmul(B[3].ap[:, ib * 128:(ib + 1) * 128], lhsT=h_.ap[:, kb, ib * 128:(ib + 1) * 128],
                                                             rhs=Wp[:, kb, 256:384], start=(kb == 0), stop=(kb == 7)),
                                      reads=[wsl, h_], writes=[B[3]])
                        fw.op(act, lambda e: e.activation(out=zsT.ap[:, tok], in_=B[2].ap, func=AF.Silu), reads=[B[2]], writes=[zsT])
                        fw.op(dve, lambda e: e.tensor_copy(out=vtk.ap[:, tt * 4:(tt + 1) * 4, :], in_=B[3].ap.rearrange("p (i n) -> p i n", n=128)),
                              reads=[B[3]], writes=[vtk])
                    units = []
                    for qt in range(4):
                        for hh in range(2):
                            nkb = 4 * qt + 4
                            grp = qt * 2 + hh
                            for kb in range(nkb - 1, -1, -1):
                                dk = kb - 4 * qt
                                units.append(dict(qt=qt, hh=hh, kb=kb, grp=grp, gi=nkb - 1 - kb, first=(kb == nkb - 1), last=(kb == 0), dk=dk,
                                                  c0=(128 * dk if dk > 0 else 0)))
                    for ui, un in enumerate(units):
                        un["u"] = ui

                    def stage_a(un):
                        u, hh, kb, c0, dk = un["u"], un["hh"], un["kb"], un["c0"], un["dk"]
                        q0 = un["qt"] * 512
                        L, E_, SP_ = B[2 + u % 2], Eb[u % 2], SPb[u % 3]
                        fw.op(pe, lambda e: e.matmul(L.ap[:, c0:512], lhsT=kT.ap[:, kb * 128:(kb + 1) * 128],
                                                     rhs=qTm[hh].ap[:, q0 + c0:q0 + 512], start=True, stop=True),
                              reads=[kT, qTm[hh]], writes=[L])
                        fw.op(act, lambda e: e.activation(out=E_.ap[:, c0:512], in_=L.ap[:, c0:512], func=AF.Exp), reads=[L], writes=[E_])
                        fw.op(act, lambda e: e.activation(out=SP_.ap[:, c0:512], in_=E_.ap[:, c0:512], func=AF.Ln, bias=1.0),
                              reads=[E_], writes=[SP_])
                        if dk >= 0:
                            fw.op(dve, lambda e: e.tensor_tensor(out=SP_.ap[:, c0:512], in0=SP_.ap[:, c0:512], in1=maskA[dk][:, c0:512],
                                                                 op=ALU.mult), reads=[SP_, constb], writes=[SP_])
                        if un["first"] and c0 > 0:
                            fw.op(dve, lambda e: e.memset(SP_.ap[:, 0:c0], 0.0), writes=[SP_])

                    def stage_b(un):
                        u, hh, kb, c0, dk, first = un["u"], un["hh"], un["kb"], un["c0"], un["dk"], un["first"]
                        q0 = un["qt"] * 512
                        ARG, SP_, w_ = B[4 + u % 2], SPb[u % 3], wb_[u % 2]
                        rb_prev, rb_next = Rb[u % 2], Rb[(u + 1) % 2]
                        fw.op(pe, lambda e: e.matmul(ARG.ap[:, c0:512], lhsT=kT.ap[:, kb * 128:(kb + 1) * 128],
                                                     rhs=qTm[hh].ap[:, q0 + c0:q0 + 512], start=True, stop=False),
                              reads=[kT, qTm[hh]], writes=[ARG])
                        fw.op(pe, lambda e: e.matmul(ARG.ap[:, c0:512], lhsT=trineg, rhs=SP_.ap[:, c0:512], start=False, stop=first),
                              reads=[constb, SP_], writes=[ARG])
                        if not first:
                            fw.op(pe, lambda e: e.matmul(ARG.ap[:, c0:512], lhsT=onesneg, rhs=rb_prev.ap[:, c0:512], start=False, stop=True),
                                  reads=[constb, rb_prev], writes=[ARG])
                        if kb > 0:
                            c1 = 128 * (dk - 1) if dk > 1 else 0
                            RB = B[un["grp"] % 2]
                            ra = 0 if first else c0
                            fw.op(pe, lambda e: e.matmul(RB.ap[:, ra:512], lhsT=identb, rhs=SP_.ap[:, ra:512], start=first, stop=True,
                                                         skip_group_check=True), reads=[constb, SP_], writes=[RB])
                            fw.op(dve, lambda e: e.tensor_copy(out=rb_next.ap[:, c1:512], in_=RB.ap[:, c1:512]), reads=[RB], writes=[rb_next])
                        fw.op(act, lambda e: e.activation(out=w_.ap[:, c0:512], in_=ARG.ap[:, c0:512], func=AF.Exp), reads=[ARG], writes=[w_])
                        if dk >= 0:
                            fw.op(dve, lambda e: e.tensor_tensor(out=w_.ap[:, c0:512], in0=w_.ap[:, c0:512], in1=maskA[dk][:, c0:512], op=ALU.mult),
                                  reads=[w_, constb], writes=[w_])
                            if c0 > 0:
                                fw.op(dve, lambda e: e.memset(w_.ap[:, 0:c0], 0.0), writes=[w_])

                    def stage_c(un):
                        u, hh, kb, qt = un["u"], un["hh"], un["kb"], un["qt"]
                        ph = hh * 64
                        OT, w_ = B[6 + hh], wb_[u % 2]
                        fw.op(pe, lambda e: e.matmul(OT.ap, lhsT=vtk.ap[:, kb, :], rhs=w_.ap, start=un["first"], stop=un["last"]),
                              reads=[vtk, w_], writes=[OT])
                        if un["last"]:
                            fw.op(dve, lambda e: e.tensor_tensor(out=YT[2][qt].ap[ph:ph + 64, hp, :], in0=OT.ap[ph:ph + 64, :],
                                                                 in1=zsT.ap[ph:ph + 64, qt * 512:(qt + 1) * 512], op=ALU.mult),
                                  reads=[OT, zsT], writes=[YT[2][qt]])

                    nU = len(units)
                    for i in range(nU + 3):
                        if i < nU:
                            stage_a(units[i])
                        if 0 <= i - 2 < nU:
                            stage_b(units[i - 2])
                        if 0 <= i - 3 < nU:
                            stage_c(units[i - 3])

            if debug and l == n_layers - 1:
                with fw.scope():
                    dsem = fw.new_sem("dsem")
                    fw.dma(pool, dsem, [(dbg_h[:, kb, :], HTf[:, kb, :]) for kb in range(8)], reads=HT)
                    for n in range(3):
                        fw.dma(pool, dsem, [(dbg_y[n][:, wb, :], YTf[n][:, wb, :]) for wb in range(4)], reads=YT[n])

            with fw.scope():
                mg = fw.sbt("m_merged", [128, 8, 1024], BF16)
                sg = [fw.sbt(f"m_sg{i}", [128, 512], BF16) for i in range(3)]
                macc = fw.sbt("m_acc", [128, 512], F32)
                mt = fw.sbt("m_t", [128, 512], F32)
                gbc = fw.sbt("m_gbc", [128, D], F32)
                tmpO = [fw.sbt(f"m_tmpO{i}", [128, 512], F32) for i in range(2)]
                dgt = [fw.sbt(f"m_dg{i}", [128, 128], F32) for i in range(2)]
                for j in range(8):
                    d_ = dgt[j % 2]
                    bk = B[6 + (j // 4)]
                    fw.op(dve, lambda e: e.tensor_scalar(out=d_.ap, in0=identf.ap, scalar1=modT[l][:, 16 + j:17 + j], scalar2=None, op0=ALU.mult),
                          reads=[identf, small], writes=[d_])
                    fw.op(pe, lambda e: e.matmul(bk.ap[:, (j % 4) * 128:(j % 4 + 1) * 128], lhsT=onesf.ap, rhs=d_.ap, start=True, stop=True),
                          reads=[onesf, d_], writes=[bk])
                for jh in range(2):
                    fw.op(act, lambda e: e.activation(out=gbc.ap[:, jh * 512:(jh + 1) * 512], in_=B[6 + jh].ap, func=AF.Identity),
                          reads=[B[6 + jh]], writes=[gbc])
                for half in range(2):
                    for j in range(8):
                        wsl = acquire("merge")
                        Wg = wsl.ap[:, 0:3072].rearrange("p (kb n) -> p kb n", n=384)
                        Wb = wsl.ap[:, 3072:4608].rearrange("p (n wb d) -> p n wb d", n=3, wb=4)
                        for t2 in range(2):
                            tt = half * 2 + t2
                            h_ = HT[tt]
                            for n in range(3):
                                for kb in range(8):
                                    fw.op(pe, lambda e: e.matmul(B[n].ap, lhsT=Wg[:, kb, n * 128:(n + 1) * 128], rhs=h_.ap[:, kb, :],
                                                                 start=(kb == 0), stop=(kb == 7)), reads=[wsl, h_], writes=[B[n]])
                                fw.op(act, lambda e: e.activation(out=sg[n].ap, in_=B[n].ap, func=AF.Sigmoid), reads=[B[n]], writes=[sg[n]])
                                y_ = YT[n][tt]
                                for wb in range(4):
                                    fw.op(pe, lambda e: e.matmul(B[3 + n].ap, lhsT=Wb[:, n, wb, :], rhs=y_.ap[:, wb, :],
                                                                 start=(wb == 0), stop=(wb == 3)), reads=[wsl, y_], writes=[B[3 + n]])
                            fw.op(dve, lambda e: e.tensor_tensor(out=macc.ap, in0=B[3].ap, in1=sg[0].ap, op=ALU.mult), reads=[B[3], sg[0]], writes=[macc])
                            fw.op(dve, lambda e: e.tensor_tensor(out=mt.ap, in0=B[4].ap, in1=sg[1].ap, op=ALU.mult), reads=[B[4], sg[1]], writes=[mt])
                            fw.op(dve, lambda e: e.tensor_tensor(out=macc.ap, in0=macc.ap, in1=mt.ap, op=ALU.add), reads=[macc, mt], writes=[macc])
                            fw.op(dve, lambda e: e.tensor_tensor(out=mt.ap, in0=B[5].ap, in1=sg[2].ap, op=ALU.mult), reads=[B[5], sg[2]], writes=[mt])
                            fw.op(dve, lambda e: e.tensor_tensor(out=mg.ap[:, j, t2 * 512:(t2 + 1) * 512], in0=macc.ap, in1=mt.ap, op=ALU.add),
                                  reads=[macc, mt], writes=[mg])
                    u = 0
                    for cc in range(2):
                        wsl = acquire("outp")
                        Wo = wsl.ap[:, 0:4096].rearrange("p (kb n) -> p kb n", n=512)
                        for i8 in range(8):
                            i = half * 8 + i8
                            bk = B[6 + u % 2]
                            to = tmpO[u % 2]
                            u += 1
                            for kb in range(8):
                                fw.op(pe, lambda e: e.matmul(bk.ap, lhsT=mg.ap[:, kb, i8 * 128:(i8 + 1) * 128], rhs=Wo[:, kb, :],
                                                             start=(kb == 0), stop=(kb == 7)), reads=[mg, wsl], writes=[bk])
                            fw.op(dve, lambda e: e.tensor_tensor(out=to.ap, in0=bk.ap, in1=gbc.ap[:, cc * 512:(cc + 1) * 512], op=ALU.mult),
                                  reads=[bk, gbc], writes=[to])
                            fw.op(dve, lambda e: e.tensor_tensor(out=XR[i].ap[:, cc * 512:(cc + 1) * 512], in0=XR[i].ap[:, cc * 512:(cc + 1) * 512],
                                                                  in1=to.ap, op=ALU.add), reads=[XR[i], to], writes=[XR[i]])

        with fw.scope():
            osem = [fw.new_sem(f"osem{i}") for i in range(2)]
            if final_norm:
                fnw = fw.sbt("fnw", [128, D], F32)
                junk = fw.sbt("junk2", [128, D], BF16)
                ot = [fw.sbt(f"ot{i}", [128, D], F32) for i in range(2)]
                fw.dma(sp, csem[3], [(fnw.ap, fnw_d)], writes=[fnw])
                for i in range(16):
                    fw.op(act, lambda e: e.activation(out=junk.ap, in_=XR[i].ap, func=AF.Square, accum_out=ss[:, i:i + 1]),
                          reads=[XR[i]], writes=[junk, small])
                fw.op(dve, lambda e: e.tensor_scalar(out=ss, in0=ss, scalar1=1.0 / D, scalar2=1e-6, op0=ALU.mult, op1=ALU.add),
                      reads=[small], writes=[small])
                fw.op(pool, lambda e: e.tensor_tensor(out=rstd, in0=ss, in1=mhalf, op=ALU.pow), reads=[small], writes=[small])
                for i in range(16):
                    o_ = ot[i % 2]
                    fw.op(dve, lambda e: e.scalar_tensor_tensor(out=o_.ap, in0=XR[i].ap, scalar=rstd[:, i:i + 1], in1=fnw.ap,
                                                                op0=ALU.mult, op1=ALU.mult), reads=[XR[i], small, fnw], writes=[o_])
                    fw.dma(sp, osem[i % 2], [(out_d[i * 128:(i + 1) * 128, :], o_.ap)], reads=[o_])
                fw.barrier(ot)
            else:
                for i in range(16):
                    fw.dma(sp, osem[i % 2], [(out_d[i * 128:(i + 1) * 128, :], XR[i].ap)], reads=[XR[i]])
                fw.barrier(XR)
        stats = {e.name: (e.n_ins, e.n_wait) for e in fw.engines}
        print("instr/waits", stats, "sems", fw.n_sems)
    return nc


def _pp(v, nb):
    return np.ascontiguousarray(np.asarray(v, np.float32).reshape(nb, 128).T)


def _host_consts():
    identf = np.eye(128, dtype=np.float32)
    cb = np.zeros((128, NCB), np.float32)
    cb[:, 0:128] = identf
    j = np.arange(128)[:, None]
    s = np.arange(128)[None, :]
    cb[:, 128:256] = -(j >= s).astype(np.float32)
    cb[:, 256:384] = -1.0
    cb[:, 384:512] = 1.0
    t64 = np.arange(64)[None, :]
    cb[:, 512:576] = ((np.arange(128)[:, None] % 64) <= t64).astype(np.float32)
    t512 = np.arange(512)[None, :]
    for dk in range(4):
        cb[:, 576 + dk * 512:576 + (dk + 1) * 512] = ((np.arange(128)[:, None] + 128 * dk) < t512).astype(np.float32)
    return identf, cb


def _vecs(b, c, ada_b, norm_w, hgrn_lb, hgrn_norm_w, conv_w, conv_b, conv_ln_w, conv_ln_b):
    v = np.zeros((128, NV), np.float32)
    for l in range(2):
        o = l * LV
        v[:, o:o + 24] = _pp(ada_b[l], 24)
        v[:, o + 24:o + 32] = _pp(norm_w[l], 8)
        v[:, o + 32:o + 36] = _pp(hgrn_lb[l], 4)
        v[:, o + 36] = np.asarray(hgrn_norm_w[l], np.float32)
        v[:, o + 37:o + 161] = np.asarray(conv_w[l], np.float32).T.reshape(4, 128, 31).transpose(1, 0, 2).reshape(128, 124)
        v[:, o + 161:o + 165] = _pp(conv_b[l], 4)
        v[:, o + 165:o + 169] = _pp(conv_ln_w[l], 4)
        v[:, o + 169:o + 173] = _pp(conv_ln_b[l], 4)
    cT = _pp(c[b], 8)
    v[:, 2 * LV:2 * LV + 16] = np.repeat(cT[:, :, None], 2, axis=2).reshape(128, 16)
    return v


def _in_maps(x, c, ada_w, ada_b, norm_w, w_in, hgrn_lb, hgrn_norm_w, conv_w, conv_b, conv_ln_w, conv_ln_b,
             w_branch, w_out, final_norm_w, cores):
    identf, cb = _host_consts()
    f = lambda a: np.ascontiguousarray(np.asarray(a, np.float32))
    ada_w, w_in, w_branch, w_out = f(ada_w), f(w_in), f(w_branch), f(w_out)
    fnw = np.ascontiguousarray(np.broadcast_to(np.asarray(final_norm_w, np.float32)[None, :], (128, D)))
    x = np.asarray(x, np.float32)
    maps = []
    for b in cores:
        maps.append({
            "x": np.ascontiguousarray(x[b]),
            "vecs": _vecs(b, np.asarray(c, np.float32), ada_b, norm_w, hgrn_lb, hgrn_norm_w, conv_w, conv_b, conv_ln_w, conv_ln_b),
            "identf": identf, "constb": cb, "fnw": fnw,
            "ada_w": ada_w, "w_in": w_in, "w_branch": w_branch, "w_out": w_out,
        })
    return maps


def kernel(x, c, ada_w, ada_b, norm_w, w_in, hgrn_lb, hgrn_norm_w, conv_w, conv_b, conv_ln_w, conv_ln_b,
           w_branch, w_out, final_norm_w):
    nc = build(2, True, False)
    maps = _in_maps(x, c, ada_w, ada_b, norm_w, w_in, hgrn_lb, hgrn_norm_w, conv_w, conv_b, conv_ln_w, conv_ln_b,
                    w_branch, w_out, final_norm_w, list(range(8)))
    res = run_bass_kernel_spmd(nc, maps, core_ids=list(range(8)))
    return np.stack([np.asarray(r["out"], np.float32) for r in res.results], axis=0)
```
